# Optimizing a Trainium2 kernel written in Bass

```python
import jax, jax.numpy as jnp
from jax import lax
import numpy as np

D_MODEL = 1024
BATCH = 4
SEQ = 8192
DEPTH = 2
DEC_BATCH = 8
DEC_SEQ = 16
PAST_LEN = 2048

CHUNK = 64
D_RNN = 1024
N_LRU_BLOCKS = 16
LRU_BLOCK = D_RNN // N_LRU_BLOCKS
LRU_C = 8.0
CONV_W = 4
N_HEADS = 16
HEAD_DIM = 64
ATT_WIDTH = N_HEADS * HEAD_DIM
Q_BLOCK = 128
D_FF = 2816
PLE_DIM = 256
RMS_EPS = 1e-6
IN_COLS = 2 * D_RNN + 3 * ATT_WIDTH + 2 * D_MODEL
SPLIT_IDX = (D_RNN, 2 * D_RNN, 2 * D_RNN + ATT_WIDTH, 2 * D_RNN + 2 * ATT_WIDTH,
             2 * D_RNN + 3 * ATT_WIDTH, 2 * D_RNN + 3 * ATT_WIDTH + D_MODEL)

kernel_name = "hawk_stickbreaking_macaron_stream_step"


def rms_norm(x, g):
    xf = x.astype(jnp.float32)
    y = xf * lax.rsqrt(jnp.mean(xf * xf, axis=-1, keepdims=True) + RMS_EPS)
    return (y * g.astype(jnp.float32)).astype(x.dtype)


def swiglu(h, w_gate, w_up, w_down):
    return (jax.nn.silu(h @ w_gate) * (h @ w_up)) @ w_down


def causal_conv(u, buf, w, b):
    S = u.shape[1]
    full = jnp.concatenate([buf.astype(u.dtype), u], axis=1)
    y = sum((full[:, j:j + S] * w[j] for j in range(CONV_W)), b)
    return y, full[:, -(CONV_W - 1):]


def rg_lru(xc, reset, h0, w_a, b_a, w_x, b_x, lam):
    B, S, _ = xc.shape
    xb = xc.reshape(B, S, N_LRU_BLOCKS, LRU_BLOCK)
    r = jax.nn.sigmoid(jnp.einsum("bsnc,ncd->bsnd", xb, w_a) + b_a).reshape(B, S, D_RNN)
    i = jax.nn.sigmoid(jnp.einsum("bsnc,ncd->bsnd", xb, w_x) + b_x).reshape(B, S, D_RNN)
    log_a = -LRU_C * jax.nn.softplus(-lam.astype(jnp.float32)) * r.astype(jnp.float32)
    rs = reset[None, :, None]
    a = jnp.where(rs, 0.0, jnp.exp(log_a))
    mult = jnp.where(rs, 1.0, jnp.sqrt(-jnp.expm1(2.0 * log_a)))
    bterm = mult * (i * xc).astype(jnp.float32)

    def combine(lhs, rhs):
        a1, b1 = lhs
        a2, b2 = rhs
        return a1 * a2, a2 * b1 + b2

    a_cum, b_cum = lax.associative_scan(combine, (a, bterm), axis=1)
    h = b_cum + a_cum * h0.astype(jnp.float32)[:, None, :]
    return h, h[:, -1]


def stick_breaking(q, k, v, q_pos, k_pos):
    B, Sq, H, dh = q.shape
    blk = min(Q_BLOCK, Sq)
    nb = Sq // blk
    qb = q.reshape(B, nb, blk, H, dh).transpose(1, 0, 2, 3, 4)
    pb = q_pos.reshape(nb, blk)
    scale = dh ** -0.5

    def one_block(args):
        qi, pi = args
        z = jnp.einsum("bqhd,bkhd->bhqk", qi, k).astype(jnp.float32) * scale
        mask = k_pos[None, :] < pi[:, None]
        log_1m = jnp.where(mask, jax.nn.log_sigmoid(-z), 0.0)
        tail = lax.cumsum(log_1m, axis=3, reverse=True) - log_1m
        w = jnp.where(mask, jnp.exp(jax.nn.log_sigmoid(z) + tail), 0.0)
        return jnp.einsum("bhqk,bkhd->bqhd", w.astype(v.dtype), v)

    o = lax.map(one_block, (qb, pb))
    return o.transpose(1, 0, 2, 3, 4).reshape(B, Sq, H * dh)


def layer(x, p, lw, k_past, v_past, h0, conv_buf, q_pos, k_pos):
    (n_ffn1, f1_g, f1_u, f1_d, n_mix, w_in, conv_w, conv_b, wa, ba, wx, bx, lam,
     w_br, w_ba, w_out, n_ffn2, f2_g, f2_u, f2_d, n_ple, ple_g, ple_p) = lw
    B, S, _ = x.shape
    x = x + 0.5 * swiglu(rms_norm(x, n_ffn1), f1_g, f1_u, f1_d)
    hn = rms_norm(x, n_mix)
    u_x, u_g, q, k, v, g_r, g_a = jnp.split(hn @ w_in, SPLIT_IDX, axis=-1)
    xc, new_conv = causal_conv(u_x, conv_buf, conv_w, conv_b)
    hseq, h_last = rg_lru(xc, q_pos == 0, h0, wa, ba, wx, bx, lam)
    y_rnn = hseq.astype(x.dtype) * jax.nn.gelu(u_g)
    q = q.reshape(B, S, N_HEADS, HEAD_DIM)
    k = k.reshape(B, S, N_HEADS, HEAD_DIM)
    v = v.reshape(B, S, N_HEADS, HEAD_DIM)
    k_all = jnp.concatenate([k_past.astype(k.dtype), k], axis=1)
    v_all = jnp.concatenate([v_past.astype(v.dtype), v], axis=1)
    o = stick_breaking(q, k_all, v_all, q_pos, k_pos)
    merged = jax.nn.sigmoid(g_r) * (y_rnn @ w_br) + jax.nn.sigmoid(g_a) * (o @ w_ba)
    x = x + merged @ w_out
    x = x + 0.5 * swiglu(rms_norm(x, n_ffn2), f2_g, f2_u, f2_d)
    x = x + jax.nn.sigmoid(rms_norm(x, n_ple) @ ple_g) * (p.astype(x.dtype) @ ple_p)
    return x, k, v, h_last, new_conv


def setup_inputs(seed: int = 0) -> dict:
    key = jax.random.key(seed)
    ks = iter(jax.random.split(key, 48))

    def nrm(shape, scale):
        return scale * jax.random.normal(next(ks), shape, jnp.float32)

    def gain(shape):
        return 1.0 + nrm(shape, 0.02)

    u = jax.random.uniform(next(ks), (DEPTH, D_RNN), jnp.float32, minval=0.9, maxval=0.999)
    s = u ** (1.0 / LRU_C)
    lru_lambda = jnp.log(s) - jnp.log1p(-s)
    return {
        "x_prompt": nrm((BATCH, SEQ, D_MODEL), 1.0),
        "x_sample": nrm((DEC_BATCH, DEC_SEQ, D_MODEL), 1.0),
        "p_prompt": nrm((DEPTH, BATCH, SEQ, PLE_DIM), 1.0),
        "p_sample": nrm((DEPTH, DEC_BATCH, DEC_SEQ, PLE_DIM), 1.0),
        "cache_k": nrm((DEPTH, DEC_BATCH, PAST_LEN, N_HEADS, HEAD_DIM), 1.0),
        "cache_v": nrm((DEPTH, DEC_BATCH, PAST_LEN, N_HEADS, HEAD_DIM), 1.0),
        "state_h": nrm((DEPTH, DEC_BATCH, D_RNN), 0.5),
        "state_conv": nrm((DEPTH, DEC_BATCH, CONV_W - 1, D_RNN), 1.0),
        "norm_ffn1": gain((DEPTH, D_MODEL)),
        "ffn1_w_gate": nrm((DEPTH, D_MODEL, D_FF), D_MODEL ** -0.5),
        "ffn1_w_up": nrm((DEPTH, D_MODEL, D_FF), D_MODEL ** -0.5),
        "ffn1_w_down": nrm((DEPTH, D_FF, D_MODEL), D_FF ** -0.5),
        "norm_mix": gain((DEPTH, D_MODEL)),
        "w_in": nrm((DEPTH, D_MODEL, IN_COLS), D_MODEL ** -0.5),
        "conv_w": nrm((DEPTH, CONV_W, D_RNN), CONV_W ** -0.5),
        "conv_b": nrm((DEPTH, D_RNN), 0.01),
        "lru_w_a": nrm((DEPTH, N_LRU_BLOCKS, LRU_BLOCK, LRU_BLOCK), LRU_BLOCK ** -0.5),
        "lru_b_a": nrm((DEPTH, N_LRU_BLOCKS, LRU_BLOCK), 0.01),
        "lru_w_x": nrm((DEPTH, N_LRU_BLOCKS, LRU_BLOCK, LRU_BLOCK), LRU_BLOCK ** -0.5),
        "lru_b_x": nrm((DEPTH, N_LRU_BLOCKS, LRU_BLOCK), 0.01),
        "lru_lambda": lru_lambda,
        "w_branch_rnn": nrm((DEPTH, D_RNN, D_MODEL), D_RNN ** -0.5),
        "w_branch_attn": nrm((DEPTH, ATT_WIDTH, D_MODEL), ATT_WIDTH ** -0.5),
        "w_out": nrm((DEPTH, D_MODEL, D_MODEL), D_MODEL ** -0.5),
        "norm_ffn2": gain((DEPTH, D_MODEL)),
        "ffn2_w_gate": nrm((DEPTH, D_MODEL, D_FF), D_MODEL ** -0.5),
        "ffn2_w_up": nrm((DEPTH, D_MODEL, D_FF), D_MODEL ** -0.5),
        "ffn2_w_down": nrm((DEPTH, D_FF, D_MODEL), D_FF ** -0.5),
        "norm_ple": gain((DEPTH, D_MODEL)),
        "ple_w_gate": nrm((DEPTH, D_MODEL, D_MODEL), D_MODEL ** -0.5),
        "ple_w_proj": nrm((DEPTH, PLE_DIM, D_MODEL), PLE_DIM ** -0.5),
        "final_norm": gain((D_MODEL,)),
    }


def reference(x_prompt, x_sample, p_prompt, p_sample, cache_k, cache_v, state_h, state_conv,
              norm_ffn1, ffn1_w_gate, ffn1_w_up, ffn1_w_down, norm_mix, w_in, conv_w, conv_b,
              lru_w_a, lru_b_a, lru_w_x, lru_b_x, lru_lambda, w_branch_rnn, w_branch_attn, w_out,
              norm_ffn2, ffn2_w_gate, ffn2_w_up, ffn2_w_down, norm_ple, ple_w_gate, ple_w_proj,
              final_norm):
    B, S = x_prompt.shape[0], x_prompt.shape[1]
    Sd = x_sample.shape[1]
    P = cache_k.shape[2]
    pos_p = jnp.arange(S, dtype=jnp.int32)
    pos_s = P + jnp.arange(Sd, dtype=jnp.int32)
    kpos_s = jnp.arange(P + Sd, dtype=jnp.int32)
    xp, xs = x_prompt, x_sample
    kp_l, vp_l, hp_l, cp_l, ks_l, vs_l, hs_l, cs_l = [], [], [], [], [], [], [], []
    for i in range(DEPTH):
        lw = (norm_ffn1[i], ffn1_w_gate[i], ffn1_w_up[i], ffn1_w_down[i], norm_mix[i], w_in[i],
              conv_w[i], conv_b[i], lru_w_a[i], lru_b_a[i], lru_w_x[i], lru_b_x[i], lru_lambda[i],
              w_branch_rnn[i], w_branch_attn[i], w_out[i], norm_ffn2[i], ffn2_w_gate[i],
              ffn2_w_up[i], ffn2_w_down[i], norm_ple[i], ple_w_gate[i], ple_w_proj[i])
        empty_kv = jnp.zeros((B, 0, N_HEADS, HEAD_DIM), xp.dtype)
        xp, kp, vp, hp, cp = layer(xp, p_prompt[i], lw, empty_kv, empty_kv,
                                   jnp.zeros((B, D_RNN), jnp.float32),
                                   jnp.zeros((B, CONV_W - 1, D_RNN), xp.dtype), pos_p, pos_p)
        xs, ks_, vs_, hs, cs = layer(xs, p_sample[i], lw, cache_k[i], cache_v[i], state_h[i],
                                     state_conv[i], pos_s, kpos_s)
        kp_l.append(kp); vp_l.append(vp); hp_l.append(hp); cp_l.append(cp)
        ks_l.append(ks_); vs_l.append(vs_); hs_l.append(hs); cs_l.append(cs)
    y_prompt = rms_norm(xp, final_norm)
    y_sample = rms_norm(xs, final_norm)
    return (y_prompt, y_sample,
            jnp.stack(kp_l), jnp.stack(vp_l), jnp.stack(hp_l), jnp.stack(cp_l),
            jnp.stack(ks_l), jnp.stack(vs_l), jnp.stack(hs_l), jnp.stack(cs_l))
```

```python
import os
import numpy as np
from contextlib import ExitStack
import concourse.bass as bass
import concourse.mybir as mybir
from concourse.bass_utils import run_bass_kernel_spmd

F32 = mybir.dt.float32
BF16 = mybir.dt.bfloat16
AF = mybir.ActivationFunctionType
ALU = mybir.AluOpType

D = 1024
DC = 8
FF = 2816
FC = 22
NH = 16
DH = 64
PLE = 256
IN_COLS = 7168
DEPTH = 2
PAST = 2048
SD = 16
EPS = 1e-6
NV_L = 96
LRU_C = 8.0


class Buf:
    __slots__ = ("lw", "rd", "ds")

    def __init__(self):
        self.lw = None
        self.rd = {}
        self.ds = None


class DSem:
    __slots__ = ("sem", "cnt")

    def __init__(self, sem):
        self.sem = sem
        self.cnt = 0


class Eng:
    def __init__(self, name, h, sem):
        self.name = name
        self.h = h
        self.sem = sem
        self.cnt = 0
        self.pending = False


class KB:
    def __init__(self, nc, stack):
        self.nc = nc
        self.E = {}
        for name, h in (("pe", nc.tensor), ("act", nc.scalar), ("dve", nc.vector),
                        ("pool", nc.gpsimd), ("sp", nc.sync)):
            sem = stack.enter_context(nc.semaphore("sem_" + name))
            self.E[name] = Eng(name, h, sem)
        self.waited = {n: {} for n in self.E}
        self.free_ds = {"sp": [DSem(stack.enter_context(nc.semaphore("dsh%d" % i))) for i in range(64)],
                        "pool": [DSem(stack.enter_context(nc.semaphore("dss%d" % i))) for i in range(24)]}
        self.all_ds = self.free_ds["sp"] + self.free_ds["pool"]
        self.phase_bufs = []

    def buf(self):
        b = Buf()
        self.phase_bufs.append(b)
        return b

    def _ds(self, b, q):
        if b.ds is None:
            b.ds = {}
        if q not in b.ds:
            b.ds[q] = self.free_ds[q].pop()
        return b.ds[q]

    def _wait(self, eng, d):
        kind, who, c = d
        if kind == "e":
            if who == eng and eng == "pe":
                return
            key = who
            sem = self.E[who].sem
        else:
            key = id(who)
            sem = who.sem
        w = self.waited[eng]
        if w.get(key, 0) >= c:
            return
        self.E[eng].h.wait_ge(sem, c)
        w[key] = c

    def _sync(self, eng, rd, wr, skip_ds=None):
        for b in rd:
            if b.lw is not None:
                self._wait(eng, b.lw)
        for b in wr:
            if b.lw is not None and not (skip_ds is not None and b.lw[0] == "d" and b.lw[1] is skip_ds):
                self._wait(eng, b.lw)
            for d in b.rd.values():
                self._wait(eng, d)

    @staticmethod
    def _rec(d, rd, wr):
        key = (d[0], d[1] if d[0] == "e" else id(d[1]))
        for b in rd:
            b.rd[key] = d
        for b in wr:
            b.lw = d
            b.rd = {}

    def op(self, eng, fn, rd=(), wr=(), inc=True):
        self._sync(eng, rd, wr)
        ins = fn(self.E[eng].h)
        E = self.E[eng]
        if inc:
            ins.then_inc(E.sem, 1)
            E.cnt += 1
            c = E.cnt
            E.pending = False
        else:
            c = E.cnt + 1
            E.pending = True
        self._rec(("e", eng, c), rd, wr)

    def dma(self, q, out, in_, rd=(), wr=(), owner=None, **kw):
        if owner is None:
            owner = wr[0] if wr else rd[0]
        ds = self._ds(owner, q)
        self._sync(q, rd, wr, skip_ds=ds)
        ins = self.E[q].h.dma_start(out=out, in_=in_, **kw)
        ins.then_inc(ds.sem, 16)
        ds.cnt += 16
        self._rec(("d", ds, ds.cnt), rd, wr)

    def barrier(self):
        for e in self.E:
            for p, P in self.E.items():
                if p != e and P.cnt > 0:
                    assert not P.pending
                    self._wait(e, ("e", p, P.cnt))
            for ds in self.all_ds:
                if ds.cnt > 0:
                    self._wait(e, ("d", ds, ds.cnt))
        for b in self.phase_bufs:
            if b.ds is not None:
                for q, ds in b.ds.items():
                    self.free_ds[q].append(ds)
                b.ds = None
        self.phase_bufs = []


class Group:
    pass


def build(S):
    nc = bass.Bass("TRN2", target_bir_lowering=False)
    stack = ExitStack()
    with stack:
        _build(nc, stack, S)
    return nc


def _build(nc, stack, S):
    def din(name, shape, dt=F32):
        return nc.dram_tensor(name, list(shape), dt, kind="ExternalInput").ap()

    def dout(name, shape, dt=F32):
        return nc.dram_tensor(name, list(shape), dt, kind="ExternalOutput").ap()

    def dscr(name, shape, dt):
        return nc.dram_tensor(name, list(shape), dt, kind="Internal").ap()

    W = {}
    W["f1g"] = din("ffn1_w_gate", [DEPTH, D, FF]); W["f1u"] = din("ffn1_w_up", [DEPTH, D, FF])
    W["f1d"] = din("ffn1_w_down", [DEPTH, FF, D])
    W["f2g"] = din("ffn2_w_gate", [DEPTH, D, FF]); W["f2u"] = din("ffn2_w_up", [DEPTH, D, FF])
    W["f2d"] = din("ffn2_w_down", [DEPTH, FF, D])
    W["win"] = din("w_in", [DEPTH, D, IN_COLS])
    W["wa"] = din("lru_w_a", [DEPTH, 16, 64, 64]); W["wx"] = din("lru_w_x", [DEPTH, 16, 64, 64])
    W["wbr"] = din("w_branch_rnn", [DEPTH, D, D]); W["wba"] = din("w_branch_attn", [DEPTH, D, D])
    W["wout"] = din("w_out", [DEPTH, D, D])
    W["pg"] = din("ple_w_gate", [DEPTH, D, D]); W["pp"] = din("ple_w_proj", [DEPTH, PLE, D])
    vecs_d = din("vecs", [128, NV_L * DEPTH + 8])
    cst_d = din("cst", [128, 3 * 128 + 4 * 512])

    groups = []
    for gname, s, past in (("p", S, 0), ("s", SD, PAST)):
        g = Group()
        g.name = gname; g.S = s; g.P = past
        g.TT = min(256, s)
        g.QT = min(512, s)
        g.KBS = min(128, s)
        g.SEG = min(2048, s)
        g.xT = din("xT" + gname, [D, s]); g.pT = din("pT" + gname, [DEPTH, PLE, s])
        if past:
            g.ckT = din("ckT", [DEPTH, D, past]); g.cv = din("cv", [DEPTH, past, D])
            g.sh = din("sh", [DEPTH, 128, DC]); g.sconv = din("sconv", [DEPTH, 128, DC, 3])
        g.yT = dout("yT" + gname, [D, s]); g.ko = dout("k" + gname, [DEPTH, s, D]); g.vo = dout("v" + gname, [DEPTH, s, D])
        g.ho = dout("h" + gname, [DEPTH, 128, DC]); g.co = dout("c" + gname, [DEPTH, 128, DC, 3])
        g.X = dscr("X" + gname, [D, s], F32); g.UX = dscr("UX" + gname, [D, s], F32)
        g.GU = dscr("GU" + gname, [D, s], BF16); g.Q = dscr("Q" + gname, [D, s], BF16)
        g.K = dscr("K" + gname, [D, past + s], BF16); g.V = dscr("V" + gname, [past + s, D], BF16)
        g.SGR = dscr("SGR" + gname, [D, s], F32); g.SGA = dscr("SGA" + gname, [D, s], F32)
        g.YR = dscr("YR" + gname, [D, s], BF16); g.OT = dscr("OT" + gname, [D, s], BF16)
        groups.append(g)

    kb = KB(nc, stack)

    uniq = [0]

    def sb(st, name, shape, dt):
        uniq[0] += 1
        return st.enter_context(nc.sbuf_tensor("s%d_%s" % (uniq[0], name), list(shape), dt))

    vecs = sb(stack, "vecs", [128, NV_L * DEPTH + 8], F32); vecs_b = kb.buf()
    cstb = sb(stack, "cstb", [128, 3 * 128 + 4 * 512], BF16); cst_b = kb.buf()
    c8 = sb(stack, "c8", [128, DC * DEPTH], F32); c8_b = kb.buf()
    PS = []
    for i in range(8):
        t = stack.enter_context(nc.psum_tensor("ps%d" % i, [128, 512], F32))
        PS.append((t, kb.buf()))
    ps_rr = [0]

    def ps_next():
        i = ps_rr[0]
        ps_rr[0] = (i + 1) % 8
        return PS[i]

    kb.dma("sp", vecs[:], vecs_d[:, :], wr=[vecs_b])
    for c0 in range(0, 3 * 128 + 4 * 512, 608):
        kb.dma("pool", cstb[:, c0:c0 + 608], cst_d[:, c0:c0 + 608], wr=[cst_b])
    negtri = cstb[:, 0:128]; negones = cstb[:, 128:256]; onesmean = cstb[:, 256:384]

    def mask_ap(r, nk, n):
        return cstb[0:nk, 384 + r * 512: 384 + r * 512 + n]

    def vcol(l, k, c):
        o = l * NV_L + k * 8 + c
        return vecs[:, o:o + 1]
    V_NF1, V_NMIX, V_CONVB, V_BA, V_BX, V_LAM, V_NF2, V_NPLE = range(8)

    def convw(l, c, j):
        o = l * NV_L + 64 + c * 4 + j
        return vecs[:, o:o + 1]

    def fin(c):
        o = DEPTH * NV_L + c
        return vecs[:, o:o + 1]

    for l in range(DEPTH):
        lam = vecs[:, l * NV_L + V_LAM * 8: l * NV_L + V_LAM * 8 + 8]
        dst = c8[:, l * 8:(l + 1) * 8]
        kb.op("act", lambda h: h.activation(out=dst, in_=lam, func=AF.Exp, scale=-1.0), rd=[vecs_b], wr=[c8_b])
        kb.op("act", lambda h: h.activation(out=dst, in_=dst, func=AF.Ln, bias=1.0), rd=[c8_b], wr=[c8_b])
        kb.op("dve", lambda h: h.tensor_scalar(out=dst, in0=dst, scalar1=-LRU_C, scalar2=None, op0=ALU.mult), rd=[c8_b], wr=[c8_b])

    def load_ws(st, specs):
        outs = [(sb(st, name, [128, kch, ncols], BF16), kb.buf(), src, kch, ncols) for name, src, kch, ncols in specs]
        with ExitStack() as lst:
            stg = [(sb(lst, "wstg%d" % i, [128, 1408], F32), kb.buf()) for i in range(4)]
            idx = 0
            for t, b, src, kch, ncols in outs:
                step = 1024 if ncols % 1024 == 0 else (1408 if ncols % 1408 == 0 else ncols)
                for c in range(kch):
                    for n0 in range(0, ncols, step):
                        s_, sb_ = stg[idx % 4]
                        kb.dma("sp", s_[:, :step], src[c * 128:(c + 1) * 128, n0:n0 + step], wr=[sb_])
                        if idx % 2 == 0:
                            kb.op("dve", lambda h: h.tensor_copy(out=t[:, c, n0:n0 + step], in_=s_[:, :step]), rd=[sb_])
                        else:
                            kb.op("act", lambda h: h.activation(out=t[:, c, n0:n0 + step], in_=s_[:, :step], func=AF.Copy), rd=[sb_])
                        idx += 1
            kb.barrier()
        return [(t, b) for t, b, _, _, _ in outs]

    def each_group(st_outer):
        for g in groups:
            with ExitStack() as gst:
                yield g, gst
                kb.barrier()

    def fm(ap2d, t0, tt):
        return ap2d.rearrange("(c p) t -> p c t", p=128)[:, :, t0:t0 + tt]

    def rmsnorm(x, xb, hn, hnb, xsq, xsqb, rstd, rstdb, T, gcol):
        kb.op("act", lambda h: h.activation(out=xsq[:, :, :T], in_=x[:, :, :T], func=AF.Square), rd=[xb], wr=[xsqb])
        pt, pb = ps_next()
        for c in range(DC):
            kb.op("pe", lambda h: h.matmul(pt[:, :T], lhsT=onesmean, rhs=xsq[:, c, :T], start=(c == 0), stop=(c == DC - 1)),
                  rd=[xsqb, cst_b], wr=[pb], inc=(c == DC - 1))
        kb.op("act", lambda h: h.activation(out=rstd[:, :T], in_=pt[:, :T], func=AF.Sqrt, bias=EPS), rd=[pb], wr=[rstdb])
        kb.op("dve", lambda h: h.reciprocal(out=rstd[:, :T], in_=rstd[:, :T]), rd=[rstdb], wr=[rstdb])
        if hn is not None:
            for c in range(DC):
                kb.op("dve", lambda h: h.scalar_tensor_tensor(out=hn[:, c, :T], in0=x[:, c, :T], scalar=gcol(c), in1=rstd[:, :T],
                                                             op0=ALU.mult, op1=ALU.mult), rd=[xb, rstdb, vecs_b], wr=[hnb])

    def mm_fm(Wt, Wb, kch, o, act, actb, T, extra_rd=()):
        pt, pb = ps_next()
        for k in range(kch):
            kb.op("pe", lambda h: h.matmul(pt[:, :T], lhsT=Wt[:, k, o * 128:(o + 1) * 128], rhs=act[:, k, :T],
                                           start=(k == 0), stop=(k == kch - 1)),
                  rd=[Wb, actb], wr=[pb], inc=(k == kch - 1))
        return pt, pb

    def phase_ffn(l, wg, wu, wd, vk, first):
        with ExitStack() as st:
            (Wg, Wgb), (Wu, Wub), (Wd, Wdb) = load_ws(st, [("Wg", wg[l], DC, FF), ("Wu", wu[l], DC, FF), ("Wd", wd[l], FC, D)])
            for g, st in each_group(st):
                T = min(512, g.S)
                nt = g.S // T
                src = g.xT if first else g.X
                xs = [(sb(st, "x%d%s" % (i, g.name), [128, DC, T], F32), kb.buf()) for i in range(2)]
                hn = sb(st, "hn" + g.name, [128, DC, T], BF16); hnb = kb.buf()
                rstd = sb(st, "rstd" + g.name, [128, T], F32); rstdb = kb.buf()
                a = sb(st, "a" + g.name, [128, FC, T], BF16); ab = kb.buf()
                xsq = a[:, 0:DC, :]; xsqb = ab
                sgs = [(sb(st, "sg%d%s" % (i, g.name), [128, T], F32), kb.buf()) for i in range(2)]
                kb.dma("sp", xs[0][0][:], fm(src, 0, T), wr=[xs[0][1]])
                for it in range(nt):
                    x, xb = xs[it % 2]
                    if it + 1 < nt:
                        kb.dma("sp", xs[(it + 1) % 2][0][:], fm(src, (it + 1) * T, T), wr=[xs[(it + 1) % 2][1]])
                    rmsnorm(x, xb, hn, hnb, xsq, xsqb, rstd, rstdb, T, lambda c: vcol(l, vk, c))
                    for f in range(FC):
                        pg_, pgb = mm_fm(Wg, Wgb, DC, f, hn, hnb, T)
                        pu_, pub = mm_fm(Wu, Wub, DC, f, hn, hnb, T)
                        sg, sgb = sgs[f % 2]
                        kb.op("act", lambda h: h.activation(out=sg[:, :T], in_=pg_[:, :T], func=AF.Silu), rd=[pgb], wr=[sgb])
                        kb.op("dve", lambda h: h.tensor_tensor(out=a[:, f, :T], in0=pu_[:, :T], in1=sg[:, :T], op=ALU.mult),
                              rd=[pub, sgb], wr=[ab])
                    for o in range(DC):
                        pd_, pdb = mm_fm(Wd, Wdb, FC, o, a, ab, T)
                        kb.op("dve", lambda h: h.scalar_tensor_tensor(out=x[:, o, :T], in0=pd_[:, :T], scalar=0.5, in1=x[:, o, :T],
                                                                     op0=ALU.mult, op1=ALU.add), rd=[pdb, xb], wr=[xb])
                    kb.dma("sp", fm(g.X, it * T, T), x[:], rd=[xb])
            kb.barrier()

    def phase_in(l):
        with ExitStack() as st:
            hn_low = sb(st, "hnlow", [128, DC, 128], BF16) if os.environ.get('KTM_HNLOW') else None
            tok_low = None
            if os.environ.get('KTM_TOKLOW'):
                tok_low = ([sb(st, "toklow%d" % i, [128, D], F32) for i in range(2)], [sb(st, "tokvlow%d" % i, [128, D], BF16) for i in range(2)])
            ((Wi, Wib),) = load_ws(st, [("Wi", W["win"][l], DC, IN_COLS)])
            for g, st in each_group(st):
                T = min(256, g.S)
                nt = g.S // T
                xs = [(sb(st, "x%d%s" % (i, g.name), [128, DC, T], F32), kb.buf()) for i in range(2)]
                hn = hn_low if hn_low is not None else sb(st, "hn" + g.name, [128, DC, T], BF16)
                hnb = kb.buf()
                hn2 = sb(st, "hn2" + g.name, [128, DC, T], BF16); hn2b = kb.buf()
                xsq = sb(st, "xsq" + g.name, [128, DC, T], BF16); xsqb = kb.buf()
                rstd = sb(st, "rstd" + g.name, [128, T], F32); rstdb = kb.buf()
                ux = sb(st, "ux" + g.name, [128, DC, T], F32); uxb = kb.buf()
                gu = sb(st, "gu" + g.name, [128, DC, T], BF16); gub = kb.buf()
                qs = sb(st, "qs" + g.name, [128, DC, T], BF16); qsb = kb.buf()
                ks = sb(st, "ks" + g.name, [128, DC, T], BF16); ksb = kb.buf()
                sgr = sb(st, "sgr" + g.name, [128, DC, T], F32); sgrb = kb.buf()
                sga = sb(st, "sga" + g.name, [128, DC, T], F32); sgab = kb.buf()
                t1 = sb(st, "t1" + g.name, [128, T], F32); t1b = kb.buf()
                t2 = sb(st, "t2" + g.name, [128, T], F32); t2b = kb.buf()
                MT = min(128, T)
                if tok_low is not None:
                    tok = [(tok_low[0][i][0:MT, :], kb.buf()) for i in range(2)]
                    tokv = [(tok_low[1][i][0:MT, :], kb.buf()) for i in range(2)]
                else:
                    tok = [(sb(st, "tok%d%s" % (i, g.name), [MT, D], F32), kb.buf()) for i in range(2)]
                    tokv = [(sb(st, "tokv%d%s" % (i, g.name), [MT, D], BF16), kb.buf()) for i in range(2)]
                kb.dma("sp", xs[0][0][:], fm(g.X, 0, T), wr=[xs[0][1]])
                for it in range(nt):
                    t0 = it * T
                    x, xb = xs[it % 2]
                    if it + 1 < nt:
                        kb.dma("sp", xs[(it + 1) % 2][0][:], fm(g.X, (it + 1) * T, T), wr=[xs[(it + 1) % 2][1]])
                    rmsnorm(x, xb, hn, hnb, xsq, xsqb, rstd, rstdb, T, lambda c: vcol(l, V_NMIX, c))
                    for c in range(DC):
                        p_, pb_ = mm_fm(Wi, Wib, DC, c, hn, hnb, T)
                        kb.op("act", lambda h: h.activation(out=ux[:, c, :T], in_=p_[:, :T], func=AF.Copy), rd=[pb_], wr=[uxb])
                    kb.dma("sp", fm(g.UX, t0, T), ux[:], rd=[uxb])
                    for c in range(0 if os.environ.get('KSKIP_GELU') else DC):
                        p_, pb_ = mm_fm(Wi, Wib, DC, 8 + c, hn, hnb, T)
                        kb.op("act", lambda h: h.activation(out=t1[:, :T], in_=p_[:, :T], func=AF.Square), rd=[pb_], wr=[t1b])
                        kb.op("dve", lambda h: h.tensor_scalar(out=t1[:, :T], in0=t1[:, :T], scalar1=0.044715, scalar2=1.0,
                                                               op0=ALU.mult, op1=ALU.add), rd=[t1b], wr=[t1b])
                        kb.op("dve", lambda h: h.tensor_tensor(out=t1[:, :T], in0=t1[:, :T], in1=p_[:, :T], op=ALU.mult),
                              rd=[t1b, pb_], wr=[t1b])
                        kb.op("act", lambda h: h.activation(out=t2[:, :T], in_=t1[:, :T], func=AF.Sigmoid, scale=1.5957691216057308),
                              rd=[t1b], wr=[t2b])
                        kb.op("dve", lambda h: h.tensor_tensor(out=gu[:, c, :T], in0=p_[:, :T], in1=t2[:, :T], op=ALU.mult),
                              rd=[t2b, pb_], wr=[gub])
                    kb.dma("sp", fm(g.GU, t0, T), gu[:], rd=[gub])
                    for c in range(DC):
                        p_, pb_ = mm_fm(Wi, Wib, DC, 16 + c, hn, hnb, T)
                        kb.op("act", lambda h: h.activation(out=qs[:, c, :T], in_=p_[:, :T], func=AF.Copy, scale=0.125), rd=[pb_], wr=[qsb])
                    kb.dma("sp", fm(g.Q, t0, T), qs[:], rd=[qsb])
                    for c in range(DC):
                        p_, pb_ = mm_fm(Wi, Wib, DC, 24 + c, hn, hnb, T)
                        kb.op("dve", lambda h: h.tensor_copy(out=ks[:, c, :T], in_=p_[:, :T]), rd=[pb_], wr=[ksb])
                    kb.dma("sp", fm(g.K, g.P + t0, T), ks[:], rd=[ksb])
                    for c in range(DC):
                        p_, pb_ = mm_fm(Wi, Wib, DC, 40 + c, hn, hnb, T)
                        kb.op("act", lambda h: h.activation(out=sgr[:, c, :T], in_=p_[:, :T], func=AF.Sigmoid), rd=[pb_], wr=[sgrb])
                    kb.dma("sp", fm(g.SGR, t0, T), sgr[:], rd=[sgrb])
                    for c in range(DC):
                        p_, pb_ = mm_fm(Wi, Wib, DC, 48 + c, hn, hnb, T)
                        kb.op("act", lambda h: h.activation(out=sga[:, c, :T], in_=p_[:, :T], func=AF.Sigmoid), rd=[pb_], wr=[sgab])
                    kb.dma("sp", fm(g.SGA, t0, T), sga[:], rd=[sgab])
                    kb.op("pool", lambda h: h.tensor_copy(out=hn2[:, :, :T], in_=hn[:, :, :T]), rd=[hnb], wr=[hn2b])
                    for m in range(T // MT):
                        for which in range(2):
                            tk, tkb = tok[which]
                            col0 = 3072 + which * 1024
                            NB = 256
                            for n in range(D // NB):
                                pt, pb = ps_next()
                                for c in range(DC):
                                    kb.op("pe", lambda h: h.matmul(pt[:MT, :NB], lhsT=hn2[:, c, m * MT:(m + 1) * MT],
                                                                   rhs=Wi[:, c, col0 + n * NB: col0 + (n + 1) * NB],
                                                                   start=(c == 0), stop=(c == DC - 1)),
                                          rd=[hn2b, Wib], wr=[pb], inc=(c == DC - 1))
                                kb.op("act", lambda h: h.activation(out=tk[:, n * NB:(n + 1) * NB], in_=pt[:MT, :NB], func=AF.Copy),
                                      rd=[pb], wr=[tkb])
                            dst = (g.ko if which == 0 else g.vo)
                            if which == 1:
                                tv, tvb = tokv[m % 2]
                                kb.op("dve", lambda h: h.tensor_copy(out=tv[:, :], in_=tk[:, :]), rd=[tkb], wr=[tvb])
                            for n in range(D // NB):
                                kb.dma("sp", dst[l, t0 + m * MT: t0 + (m + 1) * MT, n * NB:(n + 1) * NB], tk[:, n * NB:(n + 1) * NB], rd=[tkb])
                            if which == 1:
                                for n in range(D // 512):
                                    kb.dma("sp", g.V[g.P + t0 + m * MT: g.P + t0 + (m + 1) * MT, n * 512:(n + 1) * 512], tv[:, n * 512:(n + 1) * 512], rd=[tvb])
            kb.barrier()

    def phase_lru(l):
        with ExitStack() as st:
            WA = sb(st, "WA", [128, DC, 128], BF16); WAb = kb.buf()
            WX = sb(st, "WX", [128, DC, 128], BF16); WXb = kb.buf()
            kb.op("pool", lambda h: h.memset(WA[:], 0.0), wr=[WAb])
            kb.op("pool", lambda h: h.memset(WX[:], 0.0), wr=[WXb])
            for n in range(16):
                r0 = (n % 2) * 64
                kb.dma("pool", WA[r0:r0 + 64, n // 2, r0:r0 + 64], W["wa"][l, n], wr=[WAb])
                kb.dma("pool", WX[r0:r0 + 64, n // 2, r0:r0 + 64], W["wx"][l, n], wr=[WXb])
            for g, st in each_group(st):
                SEG = g.SEG
                nseg = g.S // SEG
                GT = min(512, SEG)
                NS = 2
                sl = []
                for i in range(NS):
                    d = {}
                    for nm, shp, dt in (("ux", [128, SEG + 3], F32), ("xc", [128, SEG], F32), ("xcb", [128, SEG], BF16),
                                        ("r", [128, SEG], F32), ("i", [128, SEG], F32), ("m", [128, SEG], F32),
                                        ("gu", [128, SEG], BF16), ("h", [128, SEG], F32), ("y", [128, SEG], BF16)):
                        d[nm] = (sb(st, "%s%d%s" % (nm, i, g.name), shp, dt), kb.buf())
                    sl.append(d)
                if g.P:
                    h0 = sb(st, "h0" + g.name, [128, DC], F32); h0b = kb.buf()
                    kb.dma("sp", h0[:], g.sh[l], wr=[h0b])
                k = 0
                prev = None
                for c in range(DC):
                    uxd = g.UX[c * 128:(c + 1) * 128, :]
                    for sg_ in range(nseg):
                        t0 = sg_ * SEG
                        d = sl[k % NS]; k += 1
                        ux, uxb = d["ux"]; xc, xcb_ = d["xc"]; xcb, xcbb = d["xcb"]; r, rb = d["r"]; ii, ib = d["i"]
                        m, mb = d["m"]; gu, gub = d["gu"]; hh, hb = d["h"]; y, yb = d["y"]
                        if sg_ == 0:
                            if g.P:
                                kb.dma("sp", ux[:, 0:3], g.sconv[l, :, c, :], wr=[uxb])
                            else:
                                kb.op("pool", lambda h: h.memset(ux[:, 0:3], 0.0), wr=[uxb])
                            kb.dma("sp", ux[:, 3:SEG + 3], uxd[:, 0:SEG], wr=[uxb])
                        else:
                            kb.dma("sp", ux[:, :], uxd[:, t0 - 3:t0 + SEG], wr=[uxb])
                        kb.dma("sp", gu[:], g.GU[c * 128:(c + 1) * 128, t0:t0 + SEG], wr=[gub])
                        kb.op("dve", lambda h: h.tensor_scalar(out=xc[:], in0=ux[:, 0:SEG], scalar1=convw(l, c, 0), scalar2=vcol(l, V_CONVB, c),
                                                               op0=ALU.mult, op1=ALU.add), rd=[uxb, vecs_b], wr=[xcb_])
                        for j in range(1, 4):
                            kb.op("dve", lambda h: h.scalar_tensor_tensor(out=xc[:], in0=ux[:, j:j + SEG], scalar=convw(l, c, j), in1=xc[:],
                                                                         op0=ALU.mult, op1=ALU.add), rd=[uxb, xcb_, vecs_b], wr=[xcb_])
                        kb.op("pool", lambda h: h.tensor_copy(out=xcb[:], in_=xc[:]), rd=[xcb_], wr=[xcbb])
                        if sg_ == nseg - 1:
                            kb.dma("sp", g.co[l, :, c, :], ux[:, SEG:SEG + 3], rd=[uxb])
                        for s0 in range(0, SEG, GT):
                            pt, pb = ps_next()
                            kb.op("pe", lambda h: h.matmul(pt[:, :GT], lhsT=WA[:, c, :], rhs=xcb[:, s0:s0 + GT], start=True, stop=True),
                                  rd=[WAb, xcbb], wr=[pb])
                            kb.op("act", lambda h: h.activation(out=r[:, s0:s0 + GT], in_=pt[:, :GT], func=AF.Sigmoid, bias=vcol(l, V_BA, c)),
                                  rd=[pb, vecs_b], wr=[rb])
                            pt2, pb2 = ps_next()
                            kb.op("pe", lambda h: h.matmul(pt2[:, :GT], lhsT=WX[:, c, :], rhs=xcb[:, s0:s0 + GT], start=True, stop=True),
                                  rd=[WXb, xcbb], wr=[pb2])
                            kb.op("act", lambda h: h.activation(out=ii[:, s0:s0 + GT], in_=pt2[:, :GT], func=AF.Sigmoid, bias=vcol(l, V_BX, c)),
                                  rd=[pb2, vecs_b], wr=[ib])
                        kb.op("act", lambda h: h.activation(out=r[:], in_=r[:], func=AF.Exp, scale=c8[:, l * 8 + c: l * 8 + c + 1]),
                              rd=[rb, c8_b], wr=[rb])
                        kb.op("pool", lambda h: h.tensor_tensor(out=m[:], in0=r[:], in1=r[:], op=ALU.mult), rd=[rb], wr=[mb])
                        kb.op("dve", lambda h: h.tensor_scalar_min(out=m[:], in0=m[:], scalar1=1.0), rd=[mb], wr=[mb])
                        kb.op("act", lambda h: h.activation(out=m[:], in_=m[:], func=AF.Sqrt, scale=-1.0, bias=1.0), rd=[mb], wr=[mb])
                        if g.P == 0 and sg_ == 0:
                            kb.op("pool", lambda h: h.memset(m[:, 0:1], 1.0), rd=[mb], wr=[mb])
                        kb.op("pool", lambda h: h.tensor_tensor(out=ii[:], in0=ii[:], in1=xc[:], op=ALU.mult), rd=[ib, xcb_], wr=[ib])
                        kb.op("pool", lambda h: h.tensor_tensor(out=ii[:], in0=ii[:], in1=m[:], op=ALU.mult), rd=[ib, mb], wr=[ib])
                        if sg_ == 0:
                            init = h0[:, c:c + 1] if g.P else 0.0
                            ird = [h0b] if g.P else []
                        else:
                            init = prev[0][:, SEG - 1:SEG]
                            ird = [prev[1]]
                        kb.op("dve", lambda h: h.tensor_tensor_scan(out=hh[:], data0=r[:], data1=ii[:], initial=init, op0=ALU.mult, op1=ALU.add),
                              rd=[rb, ib] + ird, wr=[hb])
                        prev = (hh, hb)
                        if sg_ == nseg - 1:
                            kb.dma("sp", g.ho[l, :, c:c + 1], hh[:, SEG - 1:SEG], rd=[hb], allow_slow_non_contiguous=True)
                        kb.op("pool", lambda h: h.tensor_tensor(out=y[:], in0=hh[:], in1=gu[:], op=ALU.mult), rd=[hb, gub], wr=[yb])
                        kb.dma("sp", g.YR[c * 128:(c + 1) * 128, t0:t0 + SEG], y[:], rd=[yb])
            kb.barrier()

    def phase_attn(l):
        with ExitStack() as st:
            for g, st in each_group(st):
                S_, P_ = g.S, g.P
                NK = P_ + S_
                QT = g.QT
                KBS = g.KBS
                npast = P_ // 128
                nkb_tot = npast + S_ // KBS
                if P_:
                    dmy = kb.buf()
                    for c in range(DC):
                        kb.dma("pool", g.K[c * 128:(c + 1) * 128, 0:P_], g.ckT[l, c * 128:(c + 1) * 128, :], owner=dmy, max_dma_last_dim=4096)
                    for r0 in range(0, P_, 128):
                        kb.dma("pool", g.V[r0:r0 + 128, :], g.cv[l, r0:r0 + 128, :], owner=dmy, max_dma_last_dim=4096)
                    dmy.lw = ("d", dmy.ds["pool"], dmy.ds["pool"].cnt)
                    kb._wait("sp", dmy.lw)
                hp_sl = []
                for i in range(2):
                    d = {}
                    d["q"] = (sb(st, "aq%d%s" % (i, g.name), [128, S_], BF16), kb.buf())
                    d["k"] = (sb(st, "ak%d%s" % (i, g.name), [128, NK], BF16), kb.buf())
                    d["v"] = (sb(st, "av%d%s" % (i, g.name), [128, nkb_tot, 128], BF16), kb.buf())
                    hp_sl.append(d)
                es = [(sb(st, "e%d%s" % (i, g.name), [128, QT], F32), kb.buf()) for i in range(6)]
                xcs = [(sb(st, "xc%d%s" % (i, g.name), [128, QT], F32), kb.buf()) for i in range(2)]
                sps = [(sb(st, "sp%d%s" % (i, g.name), [128, QT], BF16), kb.buf()) for i in range(3)]
                ws = [(sb(st, "w%d%s" % (i, g.name), [128, QT], BF16), kb.buf()) for i in range(3)]
                sacc = [[(sb(st, "sa%d%d%s" % (i, j, g.name), [128, QT], BF16), kb.buf()) for j in range(2)] for i in range(2)]
                oev = [(sb(st, "oe%d%s" % (i, g.name), [64, QT], BF16), kb.buf()) for i in range(2)]
                ZP = [PS[0], PS[1], PS[6]]; CP = [PS[2], PS[3], PS[7]]; OP = [PS[4], PS[5]]

                def load_hp(hp):
                    d = hp_sl[hp % 2]
                    kb.dma("sp", d["q"][0][:], g.Q[hp * 128:(hp + 1) * 128, :], wr=[d["q"][1]])
                    kb.dma("sp", d["k"][0][:], g.K[hp * 128:(hp + 1) * 128, :], wr=[d["k"][1]])
                    vt, vb = d["v"]
                    if P_:
                        for b0 in range(0, npast, 8):
                            nb = min(8, npast - b0)
                            kb.dma("sp", vt[:, b0:b0 + nb, :],
                                   g.V[b0 * 128:(b0 + nb) * 128, hp * 128:(hp + 1) * 128].rearrange("(b p) d -> p b d", p=128), wr=[vb])
                    nnew = S_ // KBS
                    for b0 in range(0, nnew, 8):
                        nb = min(8, nnew - b0)
                        kb.dma("sp", vt[0:KBS, npast + b0: npast + b0 + nb, :],
                               g.V[P_ + b0 * KBS: P_ + (b0 + nb) * KBS, hp * 128:(hp + 1) * 128].rearrange("(b p) d -> p b d", p=KBS), wr=[vb])

                units = []
                seqi = 0
                for hp in range(NH // 2):
                    for qt in range(S_ // QT):
                        for ab in range(2):
                            blocks = []
                            nd = QT // KBS
                            for r in range(nd - 1, -1, -1):
                                blocks.append((npast + qt * nd + r, KBS, P_ + (qt * nd + r) * KBS, r))
                            for bq in range(qt * nd - 1, -1, -1):
                                blocks.append((npast + bq, KBS, P_ + bq * KBS, None))
                            for bp in range(npast - 1, -1, -1):
                                blocks.append((bp, 128, bp * 128, None))
                            for bi, (kbi, nk, kc0, r) in enumerate(blocks):
                                units.append(dict(hp=hp, qt=qt, ab=ab, kbi=kbi, nk=nk, kc0=kc0, r=r, pos=bi, first=(bi == 0),
                                                  last=(bi == len(blocks) - 1), seq=seqi))
                            seqi += 1

                loaded = set()

                def ensure(hp):
                    if hp < NH // 2 and hp not in loaded:
                        load_hp(hp)
                        loaded.add(hp)

                def qk_slices(u):
                    d = hp_sl[u["hp"] % 2]
                    qt_, qb = d["q"]; kt_, kbf = d["k"]
                    r0 = u["ab"] * 64
                    return (qt_[r0:r0 + 64, u["qt"] * QT:(u["qt"] + 1) * QT], kt_[r0:r0 + 64, u["kc0"]:u["kc0"] + u["nk"]], qb, kbf)

                def s_qk(i, u):
                    qsl, ksl, qb, kbf = qk_slices(u)
                    zt, zb = ZP[i % 3]
                    kb.op("pe", lambda h: h.matmul(zt[:u["nk"], :QT], lhsT=ksl, rhs=qsl, start=True, stop=True), rd=[qb, kbf], wr=[zb])

                def s_e(i, u):
                    nk = u["nk"]
                    zt, zb = ZP[i % 3]
                    e, eb = es[i % 6]
                    kb.op("act", lambda h: h.activation(out=e[:nk, :], in_=zt[:nk, :QT], func=AF.Exp), rd=[zb], wr=[eb])

                def s_sp(i, u):
                    nk = u["nk"]
                    e, eb = es[i % 6]
                    sp_, spb = sps[i % 3]
                    kb.op("act", lambda h: h.activation(out=sp_[:nk, :], in_=e[:nk, :], func=AF.Ln, bias=1.0), rd=[eb], wr=[spb])
                    if u["r"] is not None:
                        kb.op("dve", lambda h: h.tensor_tensor(out=sp_[:nk, :], in0=sp_[:nk, :], in1=mask_ap(u["r"], nk, QT), op=ALU.mult),
                              rd=[spb, cst_b], wr=[spb])
                    if not u["last"]:
                        so, sob = sacc[u["seq"] % 2][u["pos"] % 2]
                        sn, snb = sacc[u["seq"] % 2][(u["pos"] + 1) % 2]
                        if u["first"]:
                            if nk < 128:
                                kb.op("pool", lambda h: h.memset(sn[:], 0.0), wr=[snb])
                            kb.op("pool", lambda h: h.tensor_copy(out=sn[:nk, :], in_=sp_[:nk, :]), rd=[spb], wr=[snb])
                        else:
                            kb.op("pool", lambda h: h.tensor_tensor(out=sn[:, :], in0=so[:, :], in1=sp_[:, :], op=ALU.add),
                                  rd=[spb, sob], wr=[snb])

                def s_ct(i, u):
                    nk = u["nk"]
                    ct, cb = CP[i % 3]
                    sp_, spb = sps[i % 3]
                    kb.op("pe", lambda h: h.matmul(ct[:nk, :QT], lhsT=negtri[0:nk, 0:nk], rhs=sp_[:nk, :], start=True, stop=u["first"]),
                          rd=[spb, cst_b], wr=[cb], inc=u["first"])
                    if not u["first"]:
                        sa, sab = sacc[u["seq"] % 2][u["pos"] % 2]
                        kb.op("pe", lambda h: h.matmul(ct[:nk, :QT], lhsT=negones[:, 0:nk], rhs=sa[:, :], start=False, stop=True),
                              rd=[sab, cst_b], wr=[cb])

                def s_xc(i, u):
                    nk = u["nk"]
                    ct, cb = CP[i % 3]
                    xc_, xcb2 = xcs[i % 2]
                    kb.op("act", lambda h: h.activation(out=xc_[:nk, :], in_=ct[:nk, :QT], func=AF.Exp), rd=[cb], wr=[xcb2])

                def s_w(i, u):
                    nk = u["nk"]
                    xc_, xcb2 = xcs[i % 2]
                    e, eb = es[i % 6]
                    w_, wb = ws[i % 3]
                    kb.op("dve", lambda h: h.tensor_tensor(out=w_[:nk, :], in0=e[:nk, :], in1=xc_[:nk, :], op=ALU.mult),
                          rd=[eb, xcb2], wr=[wb])
                    if u["r"] is not None:
                        kb.op("dve", lambda h: h.tensor_tensor(out=w_[:nk, :], in0=w_[:nk, :], in1=mask_ap(u["r"], nk, QT), op=ALU.mult),
                              rd=[wb, cst_b], wr=[wb])

                def s_pv(i, u):
                    d = hp_sl[u["hp"] % 2]
                    vt, vb = d["v"]
                    nk = u["nk"]
                    c0 = u["ab"] * 64
                    ot, ob = OP[u["seq"] % 2]
                    w_, wb = ws[i % 3]
                    kb.op("pe", lambda h: h.matmul(ot[:64, :QT], lhsT=vt[0:nk, u["kbi"], c0:c0 + 64], rhs=w_[:nk, :],
                                                   start=u["first"], stop=u["last"]), rd=[vb, wb], wr=[ob])
                    if u["last"]:
                        oe, oeb = oev[u["seq"] % 2]
                        kb.op("dve", lambda h: h.tensor_copy(out=oe[:, :], in_=ot[:64, :QT]), rd=[ob], wr=[oeb])
                        hrow = u["hp"] * 128 + c0
                        kb.dma("sp", g.OT[hrow:hrow + 64, u["qt"] * QT:(u["qt"] + 1) * QT], oe[:, :], rd=[oeb])

                n = len(units)
                ensure(0)

                def at(j):
                    return units[j] if 0 <= j < n else None
                for k in range(n + 7):
                    u = at(k)
                    u6 = at(k - 6)
                    if u6 is not None and u6["first"] and u6["qt"] == 0 and u6["ab"] == 0:
                        ensure(u6["hp"] + 1)
                    if u is not None:
                        s_qk(k, u)
                    if at(k - 4) is not None:
                        s_xc(k - 4, at(k - 4))
                    if at(k - 3) is not None:
                        s_ct(k - 3, at(k - 3))
                    if at(k - 1) is not None:
                        s_e(k - 1, at(k - 1))
                    if at(k - 5) is not None:
                        s_w(k - 5, at(k - 5))
                    if at(k - 2) is not None:
                        s_sp(k - 2, at(k - 2))
                    if at(k - 6) is not None:
                        s_pv(k - 6, at(k - 6))
                kb.barrier()

    def phase_merge(l):
        with ExitStack() as st:
            (Wr, Wrb), (Wa, Wab), (Wo, Wob) = load_ws(st, [("Wr", W["wbr"][l], DC, D), ("Wa", W["wba"][l], DC, D), ("Wo", W["wout"][l], DC, D)])
            for g, st in each_group(st):
                T = min(512, g.S)
                nt = g.S // T
                sl = []
                for i in range(2):
                    d = {}
                    for nm, dt in (("x", F32), ("yr", BF16), ("ot", BF16), ("sgr", F32), ("sga", F32)):
                        d[nm] = (sb(st, "%s%d%s" % (nm, i, g.name), [128, DC, T], dt), kb.buf())
                    sl.append(d)
                mg = sb(st, "mg" + g.name, [128, DC, T], BF16); mgb = kb.buf()
                m1s = [(sb(st, "m1%d%s" % (i, g.name), [128, T], F32), kb.buf()) for i in range(2)]
                m2s = [(sb(st, "m2%d%s" % (i, g.name), [128, T], F32), kb.buf()) for i in range(2)]

                def load(it):
                    d = sl[it % 2]
                    t0 = it * T
                    for nm, src in (("x", g.X), ("yr", g.YR), ("ot", g.OT), ("sgr", g.SGR), ("sga", g.SGA)):
                        kb.dma("sp", d[nm][0][:], fm(src, t0, T), wr=[d[nm][1]])
                load(0)
                for it in range(nt):
                    if it + 1 < nt:
                        load(it + 1)
                    d = sl[it % 2]
                    x, xb = d["x"]
                    for o in range(DC):
                        p1, p1b = mm_fm(Wr, Wrb, DC, o, d["yr"][0], d["yr"][1], T)
                        p2, p2b = mm_fm(Wa, Wab, DC, o, d["ot"][0], d["ot"][1], T)
                        m1, m1b = m1s[o % 2]; m2, m2b = m2s[o % 2]
                        kb.op("dve", lambda h: h.tensor_tensor(out=m1[:, :T], in0=p1[:, :T], in1=d["sgr"][0][:, o, :T], op=ALU.mult),
                              rd=[p1b, d["sgr"][1]], wr=[m1b])
                        kb.op("dve", lambda h: h.tensor_tensor(out=m2[:, :T], in0=p2[:, :T], in1=d["sga"][0][:, o, :T], op=ALU.mult),
                              rd=[p2b, d["sga"][1]], wr=[m2b])
                        kb.op("pool", lambda h: h.tensor_tensor(out=mg[:, o, :T], in0=m1[:, :T], in1=m2[:, :T], op=ALU.add),
                              rd=[m1b, m2b], wr=[mgb])
                    for o in range(DC):
                        p3, p3b = mm_fm(Wo, Wob, DC, o, mg, mgb, T)
                        kb.op("dve", lambda h: h.tensor_tensor(out=x[:, o, :T], in0=p3[:, :T], in1=x[:, o, :T], op=ALU.add),
                              rd=[p3b, xb], wr=[xb])
                    kb.dma("sp", fm(g.X, it * T, T), x[:], rd=[xb])
            kb.barrier()

    def phase_ple(l):
        last = (l == DEPTH - 1)
        with ExitStack() as st:
            (Pg, Pgb), (Pp, Ppb) = load_ws(st, [("Pg", W["pg"][l], DC, D), ("Pp", W["pp"][l], 2, D)])
            for g, st in each_group(st):
                T = min(512, g.S)
                nt = g.S // T
                xs = [(sb(st, "x%d%s" % (i, g.name), [128, DC, T], F32), kb.buf()) for i in range(2)]
                pbs = [(sb(st, "pb%d%s" % (i, g.name), [128, 2, T], BF16), kb.buf()) for i in range(2)]
                hn = sb(st, "hn" + g.name, [128, DC, T], BF16); hnb = kb.buf()
                xsq = sb(st, "xsq" + g.name, [128, DC, T], BF16); xsqb = kb.buf()
                rstd = sb(st, "rstd" + g.name, [128, T], F32); rstdb = kb.buf()
                sgs = [(sb(st, "sg%d%s" % (i, g.name), [128, T], F32), kb.buf()) for i in range(2)]
                yo = sb(st, "yo" + g.name, [128, DC, T], F32); yob = kb.buf()

                def load(it):
                    t0 = it * T
                    kb.dma("sp", xs[it % 2][0][:], fm(g.X, t0, T), wr=[xs[it % 2][1]])
                    kb.dma("pool", pbs[it % 2][0][:], fm(g.pT[l], t0, T), wr=[pbs[it % 2][1]])
                load(0)
                for it in range(nt):
                    if it + 1 < nt:
                        load(it + 1)
                    x, xb = xs[it % 2]
                    pb_, pbb = pbs[it % 2]
                    rmsnorm(x, xb, hn, hnb, xsq, xsqb, rstd, rstdb, T, lambda c: vcol(l, V_NPLE, c))
                    for o in range(DC):
                        p1, p1b = mm_fm(Pg, Pgb, DC, o, hn, hnb, T)
                        p2, p2b = mm_fm(Pp, Ppb, 2, o, pb_, pbb, T)
                        sg, sgb = sgs[o % 2]
                        kb.op("act", lambda h: h.activation(out=sg[:, :T], in_=p1[:, :T], func=AF.Sigmoid), rd=[p1b], wr=[sgb])
                        kb.op("dve", lambda h: h.tensor_tensor(out=sg[:, :T], in0=p2[:, :T], in1=sg[:, :T], op=ALU.mult), rd=[p2b, sgb], wr=[sgb])
                        kb.op("pool", lambda h: h.tensor_tensor(out=x[:, o, :T], in0=x[:, o, :T], in1=sg[:, :T], op=ALU.add), rd=[xb, sgb], wr=[xb])
                    if not last:
                        kb.dma("sp", fm(g.X, it * T, T), x[:], rd=[xb])
                    else:
                        rmsnorm(x, xb, None, None, xsq, xsqb, rstd, rstdb, T, None)
                        for c in range(DC):
                            kb.op("dve", lambda h: h.scalar_tensor_tensor(out=yo[:, c, :T], in0=x[:, c, :T], scalar=fin(c), in1=rstd[:, :T],
                                                                         op0=ALU.mult, op1=ALU.mult), rd=[xb, rstdb, vecs_b], wr=[yob])
                        kb.dma("sp", fm(g.yT, it * T, T), yo[:], rd=[yob])
            kb.barrier()

    import os
    nph = int(os.environ.get("KDEBUG_PHASES", "999"))
    plist = []
    for l in range(DEPTH):
        plist += [lambda l=l: phase_ffn(l, W["f1g"], W["f1u"], W["f1d"], V_NF1, first=(l == 0)),
                  lambda l=l: phase_in(l), lambda l=l: phase_lru(l), lambda l=l: phase_attn(l), lambda l=l: phase_merge(l),
                  lambda l=l: phase_ffn(l, W["f2g"], W["f2u"], W["f2d"], V_NF2, first=False), lambda l=l: phase_ple(l)]
    for ph in plist[:nph]:
        ph()
    kb.barrier()


def _consts():
    c = np.zeros((128, 3 * 128 + 4 * 512), np.float32)
    j = np.arange(128)[:, None]
    s = np.arange(128)[None, :]
    c[:, 0:128] = -(j >= s).astype(np.float32)
    c[:, 128:256] = -1.0
    c[:, 256:384] = 1.0 / 1024.0
    t = np.arange(512)[None, :]
    for r in range(4):
        c[:, 384 + r * 512: 384 + (r + 1) * 512] = (t > (r * 128 + j)).astype(np.float32)
    return c


def _pc(v):
    return np.ascontiguousarray(np.asarray(v, np.float32).reshape(8, 128).T)


_NC_CACHE = {}


def kernel(**inp):
    inp = {k: np.asarray(v) for k, v in inp.items()}
    xp = inp["x_prompt"]
    B, S = xp.shape[0], xp.shape[1]
    if S not in _NC_CACHE:
        _NC_CACHE[S] = build(S)
    nc = _NC_CACHE[S]
    vecs = np.zeros((128, NV_L * DEPTH + 8), np.float32)
    for l in range(DEPTH):
        o = l * NV_L
        for k, nm in enumerate(("norm_ffn1", "norm_mix", "conv_b", "lru_b_a", "lru_b_x", "lru_lambda", "norm_ffn2", "norm_ple")):
            vecs[:, o + k * 8: o + k * 8 + 8] = _pc(inp[nm][l].reshape(-1))
        cw = inp["conv_w"][l]
        vecs[:, o + 64: o + 96] = np.ascontiguousarray(cw.reshape(4, 8, 128).transpose(2, 1, 0)).reshape(128, 32)
    vecs[:, DEPTH * NV_L:] = _pc(inp["final_norm"])
    cst = _consts()
    shared = {"vecs": vecs, "cst": cst}
    for nm in ("ffn1_w_gate", "ffn1_w_up", "ffn1_w_down", "ffn2_w_gate", "ffn2_w_up", "ffn2_w_down", "w_in", "lru_w_a", "lru_w_x",
               "w_branch_rnn", "w_branch_attn", "w_out", "ple_w_gate", "ple_w_proj"):
        shared[nm] = np.ascontiguousarray(inp[nm], dtype=np.float32)
    n_cores = 8
    in_maps = []
    for c in range(n_cores):
        b = (c * B) // n_cores
        m = dict(shared)
        m["xTp"] = np.ascontiguousarray(xp[b].T)
        m["pTp"] = np.ascontiguousarray(inp["p_prompt"][:, b].transpose(0, 2, 1))
        m["xTs"] = np.ascontiguousarray(inp["x_sample"][c].T)
        m["pTs"] = np.ascontiguousarray(inp["p_sample"][:, c].transpose(0, 2, 1))
        m["ckT"] = np.ascontiguousarray(inp["cache_k"][:, c].reshape(DEPTH, PAST, D).transpose(0, 2, 1))
        m["cv"] = np.ascontiguousarray(inp["cache_v"][:, c].reshape(DEPTH, PAST, D))
        m["sh"] = np.ascontiguousarray(inp["state_h"][:, c].reshape(DEPTH, 8, 128).transpose(0, 2, 1))
        m["sconv"] = np.ascontiguousarray(inp["state_conv"][:, c].reshape(DEPTH, 3, 8, 128).transpose(0, 3, 2, 1))
        in_maps.append(m)
    res = run_bass_kernel_spmd(nc, in_maps, core_ids=list(range(n_cores)))
    R = res.results
    DB = inp["x_sample"].shape[0]
    y_p = np.zeros((B, S, D), np.float32); k_p = np.zeros((DEPTH, B, S, NH, DH), np.float32); v_p = np.zeros_like(k_p)
    h_p = np.zeros((DEPTH, B, D), np.float32); c_p = np.zeros((DEPTH, B, 3, D), np.float32)
    y_s = np.zeros((DB, SD, D), np.float32); k_s = np.zeros((DEPTH, DB, SD, NH, DH), np.float32); v_s = np.zeros_like(k_s)
    h_s = np.zeros((DEPTH, DB, D), np.float32); c_s = np.zeros((DEPTH, DB, 3, D), np.float32)
    for c in range(n_cores):
        r = R[c]
        b = c // 2
        if c % 2 == 0:
            y_p[b] = r["yTp"].T
            k_p[:, b] = r["kp"].reshape(DEPTH, S, NH, DH)
            v_p[:, b] = r["vp"].reshape(DEPTH, S, NH, DH)
            h_p[:, b] = r["hp"].transpose(0, 2, 1).reshape(DEPTH, D)
            c_p[:, b] = r["cp"].transpose(0, 3, 2, 1).reshape(DEPTH, 3, D)
        y_s[c] = r["yTs"].T
        k_s[:, c] = r["ks"].reshape(DEPTH, SD, NH, DH)
        v_s[:, c] = r["vs"].reshape(DEPTH, SD, NH, DH)
        h_s[:, c] = r["hs"].transpose(0, 2, 1).reshape(DEPTH, D)
        c_s[:, c] = r["cs"].transpose(0, 3, 2, 1).reshape(DEPTH, 3, D)
    return (y_p, y_s, k_p, v_p, h_p, c_p, k_s, v_s, h_s, c_s)
```

```python
import os
import numpy as np
from contextlib import ExitStack
import concourse.bass as bass
import concourse.mybir as mybir
from concourse.bass_utils import run_bass_kernel_spmd

F32 = mybir.dt.float32
BF16 = mybir.dt.bfloat16
AF = mybir.ActivationFunctionType
ALU = mybir.AluOpType

D = 1024
DC = 8
FF = 2816
FC = 22
NH = 16
DH = 64
PLE = 256
IN_COLS = 7168
DEPTH = 2
PAST = 2048
SD = 16
EPS = 1e-6
NV_L = 96
LRU_C = 8.0


class Buf:
    __slots__ = ("lw", "rd", "ds")

    def __init__(self):
        self.lw = None
        self.rd = {}
        self.ds = None


class DSem:
    __slots__ = ("sem", "cnt")

    def __init__(self, sem):
        self.sem = sem
        self.cnt = 0


class Eng:
    def __init__(self, name, h, sem):
        self.name = name
        self.h = h
        self.sem = sem
        self.cnt = 0
        self.pending = False


class KB:
    def __init__(self, nc, stack):
        self.nc = nc
        self.E = {}
        for name, h in (("pe", nc.tensor), ("act", nc.scalar), ("dve", nc.vector),
                        ("pool", nc.gpsimd), ("sp", nc.sync)):
            sem = stack.enter_context(nc.semaphore("sem_" + name))
            self.E[name] = Eng(name, h, sem)
        self.waited = {n: {} for n in self.E}
        self.free_ds = {"sp": [DSem(stack.enter_context(nc.semaphore("dsh%d" % i))) for i in range(64)],
                        "pool": [DSem(stack.enter_context(nc.semaphore("dss%d" % i))) for i in range(24)]}
        self.all_ds = self.free_ds["sp"] + self.free_ds["pool"]
        self.phase_bufs = []

    def buf(self):
        b = Buf()
        self.phase_bufs.append(b)
        return b

    def _ds(self, b, q):
        if b.ds is None:
            b.ds = {}
        if q not in b.ds:
            b.ds[q] = self.free_ds[q].pop()
        return b.ds[q]

    def _wait(self, eng, d):
        kind, who, c = d
        if kind == "e":
            if who == eng and eng == "pe":
                return
            key = who
            sem = self.E[who].sem
        else:
            key = id(who)
            sem = who.sem
        w = self.waited[eng]
        if w.get(key, 0) >= c:
            return
        self.E[eng].h.wait_ge(sem, c)
        w[key] = c

    def _sync(self, eng, rd, wr, skip_ds=None):
        for b in rd:
            if b.lw is not None:
                self._wait(eng, b.lw)
        for b in wr:
            if b.lw is not None and not (skip_ds is not None and b.lw[0] == "d" and b.lw[1] is skip_ds):
                self._wait(eng, b.lw)
            for d in b.rd.values():
                self._wait(eng, d)

    @staticmethod
    def _rec(d, rd, wr):
        key = (d[0], d[1] if d[0] == "e" else id(d[1]))
        for b in rd:
            b.rd[key] = d
        for b in wr:
            b.lw = d
            b.rd = {}

    def op(self, eng, fn, rd=(), wr=(), inc=True):
        self._sync(eng, rd, wr)
        ins = fn(self.E[eng].h)
        E = self.E[eng]
        if inc:
            ins.then_inc(E.sem, 1)
            E.cnt += 1
            c = E.cnt
            E.pending = False
        else:
            c = E.cnt + 1
            E.pending = True
        self._rec(("e", eng, c), rd, wr)

    def dma(self, q, out, in_, rd=(), wr=(), owner=None, **kw):
        if owner is None:
            owner = wr[0] if wr else rd[0]
        ds = self._ds(owner, q)
        self._sync(q, rd, wr, skip_ds=ds)
        ins = self.E[q].h.dma_start(out=out, in_=in_, **kw)
        ins.then_inc(ds.sem, 16)
        ds.cnt += 16
        self._rec(("d", ds, ds.cnt), rd, wr)

    def barrier(self):
        for e in self.E:
            for p, P in self.E.items():
                if p != e and P.cnt > 0:
                    assert not P.pending
                    self._wait(e, ("e", p, P.cnt))
            for ds in self.all_ds:
                if ds.cnt > 0:
                    self._wait(e, ("d", ds, ds.cnt))
        for b in self.phase_bufs:
            if b.ds is not None:
                for q, ds in b.ds.items():
                    self.free_ds[q].append(ds)
                b.ds = None
        self.phase_bufs = []


class Group:
    pass


def build(S):
    nc = bass.Bass("TRN2", target_bir_lowering=False)
    stack = ExitStack()
    with stack:
        _build(nc, stack, S)
    return nc


def _build(nc, stack, S):
    def din(name, shape, dt=F32):
        return nc.dram_tensor(name, list(shape), dt, kind="ExternalInput").ap()

    def dout(name, shape, dt=F32):
        return nc.dram_tensor(name, list(shape), dt, kind="ExternalOutput").ap()

    def dscr(name, shape, dt):
        return nc.dram_tensor(name, list(shape), dt, kind="Internal").ap()

    W = {}
    W["f1g"] = din("ffn1_w_gate", [DEPTH, D, FF]); W["f1u"] = din("ffn1_w_up", [DEPTH, D, FF])
    W["f1d"] = din("ffn1_w_down", [DEPTH, FF, D])
    W["f2g"] = din("ffn2_w_gate", [DEPTH, D, FF]); W["f2u"] = din("ffn2_w_up", [DEPTH, D, FF])
    W["f2d"] = din("ffn2_w_down", [DEPTH, FF, D])
    W["win"] = din("w_in", [DEPTH, D, IN_COLS])
    W["wa"] = din("lru_w_a", [DEPTH, 16, 64, 64]); W["wx"] = din("lru_w_x", [DEPTH, 16, 64, 64])
    W["wbr"] = din("w_branch_rnn", [DEPTH, D, D]); W["wba"] = din("w_branch_attn", [DEPTH, D, D])
    W["wout"] = din("w_out", [DEPTH, D, D])
    W["pg"] = din("ple_w_gate", [DEPTH, D, D]); W["pp"] = din("ple_w_proj", [DEPTH, PLE, D])
    vecs_d = din("vecs", [128, NV_L * DEPTH + 8])
    cst_d = din("cst", [128, 3 * 128 + 4 * 512])

    groups = []
    for gname, s, past in (("p", S, 0), ("s", SD, PAST)):
        g = Group()
        g.name = gname; g.S = s; g.P = past
        g.TT = min(256, s)
        g.QT = min(512, s)
        g.KBS = min(128, s)
        g.SEG = min(2048, s)
        g.xT = din("xT" + gname, [D, s]); g.pT = din("pT" + gname, [DEPTH, PLE, s])
        if past:
            g.ckT = din("ckT", [DEPTH, D, past]); g.cv = din("cv", [DEPTH, past, D])
            g.sh = din("sh", [DEPTH, 128, DC]); g.sconv = din("sconv", [DEPTH, 128, DC, 3])
        g.yT = dout("yT" + gname, [D, s]); g.ko = dout("k" + gname, [DEPTH, s, D]); g.vo = dout("v" + gname, [DEPTH, s, D])
        g.ho = dout("h" + gname, [DEPTH, 128, DC]); g.co = dout("c" + gname, [DEPTH, 128, DC, 3])
        g.X = dscr("X" + gname, [D, s], F32); g.UX = dscr("UX" + gname, [D, s], F32)
        g.GU = dscr("GU" + gname, [D, s], BF16); g.Q = dscr("Q" + gname, [D, s], BF16)
        g.K = dscr("K" + gname, [D, past + s], BF16); g.V = dscr("V" + gname, [past + s, D], BF16)
        g.SGR = dscr("SGR" + gname, [D, s], F32); g.SGA = dscr("SGA" + gname, [D, s], F32)
        g.YR = dscr("YR" + gname, [D, s], BF16); g.OT = dscr("OT" + gname, [D, s], BF16)
        groups.append(g)

    kb = KB(nc, stack)

    uniq = [0]

    def sb(st, name, shape, dt):
        uniq[0] += 1
        return st.enter_context(nc.sbuf_tensor("s%d_%s" % (uniq[0], name), list(shape), dt))

    vecs = sb(stack, "vecs", [128, NV_L * DEPTH + 8], F32); vecs_b = kb.buf()
    cstb = sb(stack, "cstb", [128, 3 * 128 + 4 * 512], BF16); cst_b = kb.buf()
    c8 = sb(stack, "c8", [128, DC * DEPTH], F32); c8_b = kb.buf()
    PS = []
    for i in range(8):
        t = stack.enter_context(nc.psum_tensor("ps%d" % i, [128, 512], F32))
        PS.append((t, kb.buf()))
    ps_rr = [0]

    def ps_next():
        i = ps_rr[0]
        ps_rr[0] = (i + 1) % 8
        return PS[i]

    kb.dma("sp", vecs[:], vecs_d[:, :], wr=[vecs_b])
    for c0 in range(0, 3 * 128 + 4 * 512, 608):
        kb.dma("pool", cstb[:, c0:c0 + 608], cst_d[:, c0:c0 + 608], wr=[cst_b])
    negtri = cstb[:, 0:128]; negones = cstb[:, 128:256]; onesmean = cstb[:, 256:384]

    def mask_ap(r, nk, n):
        return cstb[0:nk, 384 + r * 512: 384 + r * 512 + n]

    def vcol(l, k, c):
        o = l * NV_L + k * 8 + c
        return vecs[:, o:o + 1]
    V_NF1, V_NMIX, V_CONVB, V_BA, V_BX, V_LAM, V_NF2, V_NPLE = range(8)

    def convw(l, c, j):
        o = l * NV_L + 64 + c * 4 + j
        return vecs[:, o:o + 1]

    def fin(c):
        o = DEPTH * NV_L + c
        return vecs[:, o:o + 1]

    for l in range(DEPTH):
        lam = vecs[:, l * NV_L + V_LAM * 8: l * NV_L + V_LAM * 8 + 8]
        dst = c8[:, l * 8:(l + 1) * 8]
        kb.op("act", lambda h: h.activation(out=dst, in_=lam, func=AF.Exp, scale=-1.0), rd=[vecs_b], wr=[c8_b])
        kb.op("act", lambda h: h.activation(out=dst, in_=dst, func=AF.Ln, bias=1.0), rd=[c8_b], wr=[c8_b])
        kb.op("dve", lambda h: h.tensor_scalar(out=dst, in0=dst, scalar1=-LRU_C, scalar2=None, op0=ALU.mult), rd=[c8_b], wr=[c8_b])

    def load_w(st, name, src, kchunks, ncols, q="pool"):
        t = sb(st, name, [128, kchunks, ncols], BF16)
        b = kb.buf()
        step = 1024 if ncols % 1024 == 0 else (1408 if ncols % 1408 == 0 else ncols)
        for c in range(kchunks):
            for n0 in range(0, ncols, step):
                kb.dma(q, t[:, c, n0:n0 + step], src[c * 128:(c + 1) * 128, n0:n0 + step], wr=[b])
        return t, b

    def each_group(st_outer):
        for g in groups:
            with ExitStack() as gst:
                yield g, gst
                kb.barrier()

    def fm(ap2d, t0, tt):
        return ap2d.rearrange("(c p) t -> p c t", p=128)[:, :, t0:t0 + tt]

    def rmsnorm(x, xb, hn, hnb, xsq, xsqb, rstd, rstdb, T, gcol):
        kb.op("act", lambda h: h.activation(out=xsq[:, :, :T], in_=x[:, :, :T], func=AF.Square), rd=[xb], wr=[xsqb])
        pt, pb = ps_next()
        for c in range(DC):
            kb.op("pe", lambda h: h.matmul(pt[:, :T], lhsT=onesmean, rhs=xsq[:, c, :T], start=(c == 0), stop=(c == DC - 1)),
                  rd=[xsqb, cst_b], wr=[pb], inc=(c == DC - 1))
        kb.op("act", lambda h: h.activation(out=rstd[:, :T], in_=pt[:, :T], func=AF.Sqrt, bias=EPS), rd=[pb], wr=[rstdb])
        kb.op("dve", lambda h: h.reciprocal(out=rstd[:, :T], in_=rstd[:, :T]), rd=[rstdb], wr=[rstdb])
        if hn is not None:
            for c in range(DC):
                kb.op("dve", lambda h: h.scalar_tensor_tensor(out=hn[:, c, :T], in0=x[:, c, :T], scalar=gcol(c), in1=rstd[:, :T],
                                                             op0=ALU.mult, op1=ALU.mult), rd=[xb, rstdb, vecs_b], wr=[hnb])

    def mm_fm(Wt, Wb, kch, o, act, actb, T, extra_rd=()):
        pt, pb = ps_next()
        for k in range(kch):
            kb.op("pe", lambda h: h.matmul(pt[:, :T], lhsT=Wt[:, k, o * 128:(o + 1) * 128], rhs=act[:, k, :T],
                                           start=(k == 0), stop=(k == kch - 1)),
                  rd=[Wb, actb], wr=[pb], inc=(k == kch - 1))
        return pt, pb

    def phase_ffn(l, wg, wu, wd, vk, first):
        with ExitStack() as st:
            Wg, Wgb = load_w(st, "Wg", wg[l], DC, FF)
            Wu, Wub = load_w(st, "Wu", wu[l], DC, FF)
            Wd, Wdb = load_w(st, "Wd", wd[l], FC, D)
            for g, st in each_group(st):
                T = min(512, g.S)
                nt = g.S // T
                src = g.xT if first else g.X
                xs = [(sb(st, "x%d%s" % (i, g.name), [128, DC, T], F32), kb.buf()) for i in range(2)]
                hn = sb(st, "hn" + g.name, [128, DC, T], BF16); hnb = kb.buf()
                rstd = sb(st, "rstd" + g.name, [128, T], F32); rstdb = kb.buf()
                a = sb(st, "a" + g.name, [128, FC, T], BF16); ab = kb.buf()
                xsq = a[:, 0:DC, :]; xsqb = ab
                sgs = [(sb(st, "sg%d%s" % (i, g.name), [128, T], F32), kb.buf()) for i in range(2)]
                kb.dma("sp", xs[0][0][:], fm(src, 0, T), wr=[xs[0][1]])
                for it in range(nt):
                    x, xb = xs[it % 2]
                    if it + 1 < nt:
                        kb.dma("sp", xs[(it + 1) % 2][0][:], fm(src, (it + 1) * T, T), wr=[xs[(it + 1) % 2][1]])
                    rmsnorm(x, xb, hn, hnb, xsq, xsqb, rstd, rstdb, T, lambda c: vcol(l, vk, c))
                    for f in range(FC):
                        pg_, pgb = mm_fm(Wg, Wgb, DC, f, hn, hnb, T)
                        pu_, pub = mm_fm(Wu, Wub, DC, f, hn, hnb, T)
                        sg, sgb = sgs[f % 2]
                        kb.op("act", lambda h: h.activation(out=sg[:, :T], in_=pg_[:, :T], func=AF.Silu), rd=[pgb], wr=[sgb])
                        kb.op("dve", lambda h: h.tensor_tensor(out=a[:, f, :T], in0=pu_[:, :T], in1=sg[:, :T], op=ALU.mult),
                              rd=[pub, sgb], wr=[ab])
                    for o in range(DC):
                        pd_, pdb = mm_fm(Wd, Wdb, FC, o, a, ab, T)
                        kb.op("dve", lambda h: h.scalar_tensor_tensor(out=x[:, o, :T], in0=pd_[:, :T], scalar=0.5, in1=x[:, o, :T],
                                                                     op0=ALU.mult, op1=ALU.add), rd=[pdb, xb], wr=[xb])
                    kb.dma("sp", fm(g.X, it * T, T), x[:], rd=[xb])
            kb.barrier()

    def phase_in(l):
        with ExitStack() as st:
            hn_low = sb(st, "hnlow", [128, DC, 128], BF16) if os.environ.get('KTM_HNLOW') else None
            tok_low = None
            if os.environ.get('KTM_TOKLOW'):
                tok_low = ([sb(st, "toklow%d" % i, [128, D], F32) for i in range(2)], [sb(st, "tokvlow%d" % i, [128, D], BF16) for i in range(2)])
            Wi, Wib = load_w(st, "Wi", W["win"][l], DC, IN_COLS)
            for g, st in each_group(st):
                T = min(256, g.S)
                nt = g.S // T
                xs = [(sb(st, "x%d%s" % (i, g.name), [128, DC, T], F32), kb.buf()) for i in range(2)]
                hn = hn_low if hn_low is not None else sb(st, "hn" + g.name, [128, DC, T], BF16)
                hnb = kb.buf()
                hn2 = sb(st, "hn2" + g.name, [128, DC, T], BF16); hn2b = kb.buf()
                xsq = sb(st, "xsq" + g.name, [128, DC, T], BF16); xsqb = kb.buf()
                rstd = sb(st, "rstd" + g.name, [128, T], F32); rstdb = kb.buf()
                ux = sb(st, "ux" + g.name, [128, DC, T], F32); uxb = kb.buf()
                gu = sb(st, "gu" + g.name, [128, DC, T], BF16); gub = kb.buf()
                qs = sb(st, "qs" + g.name, [128, DC, T], BF16); qsb = kb.buf()
                ks = sb(st, "ks" + g.name, [128, DC, T], BF16); ksb = kb.buf()
                sgr = sb(st, "sgr" + g.name, [128, DC, T], F32); sgrb = kb.buf()
                sga = sb(st, "sga" + g.name, [128, DC, T], F32); sgab = kb.buf()
                t1 = sb(st, "t1" + g.name, [128, T], F32); t1b = kb.buf()
                t2 = sb(st, "t2" + g.name, [128, T], F32); t2b = kb.buf()
                MT = min(128, T)
                if tok_low is not None:
                    tok = [(tok_low[0][i][0:MT, :], kb.buf()) for i in range(2)]
                    tokv = [(tok_low[1][i][0:MT, :], kb.buf()) for i in range(2)]
                else:
                    tok = [(sb(st, "tok%d%s" % (i, g.name), [MT, D], F32), kb.buf()) for i in range(2)]
                    tokv = [(sb(st, "tokv%d%s" % (i, g.name), [MT, D], BF16), kb.buf()) for i in range(2)]
                kb.dma("sp", xs[0][0][:], fm(g.X, 0, T), wr=[xs[0][1]])
                for it in range(nt):
                    t0 = it * T
                    x, xb = xs[it % 2]
                    if it + 1 < nt:
                        kb.dma("sp", xs[(it + 1) % 2][0][:], fm(g.X, (it + 1) * T, T), wr=[xs[(it + 1) % 2][1]])
                    rmsnorm(x, xb, hn, hnb, xsq, xsqb, rstd, rstdb, T, lambda c: vcol(l, V_NMIX, c))
                    for c in range(DC):
                        p_, pb_ = mm_fm(Wi, Wib, DC, c, hn, hnb, T)
                        kb.op("act", lambda h: h.activation(out=ux[:, c, :T], in_=p_[:, :T], func=AF.Copy), rd=[pb_], wr=[uxb])
                    kb.dma("sp", fm(g.UX, t0, T), ux[:], rd=[uxb])
                    for c in range(0 if os.environ.get('KSKIP_GELU') else DC):
                        p_, pb_ = mm_fm(Wi, Wib, DC, 8 + c, hn, hnb, T)
                        kb.op("act", lambda h: h.activation(out=t1[:, :T], in_=p_[:, :T], func=AF.Square), rd=[pb_], wr=[t1b])
                        kb.op("dve", lambda h: h.tensor_scalar(out=t1[:, :T], in0=t1[:, :T], scalar1=0.044715, scalar2=1.0,
                                                               op0=ALU.mult, op1=ALU.add), rd=[t1b], wr=[t1b])
                        kb.op("dve", lambda h: h.tensor_tensor(out=t1[:, :T], in0=t1[:, :T], in1=p_[:, :T], op=ALU.mult),
                              rd=[t1b, pb_], wr=[t1b])
                        kb.op("act", lambda h: h.activation(out=t2[:, :T], in_=t1[:, :T], func=AF.Sigmoid, scale=1.5957691216057308),
                              rd=[t1b], wr=[t2b])
                        kb.op("dve", lambda h: h.tensor_tensor(out=gu[:, c, :T], in0=p_[:, :T], in1=t2[:, :T], op=ALU.mult),
                              rd=[t2b, pb_], wr=[gub])
                    kb.dma("sp", fm(g.GU, t0, T), gu[:], rd=[gub])
                    for c in range(DC):
                        p_, pb_ = mm_fm(Wi, Wib, DC, 16 + c, hn, hnb, T)
                        kb.op("act", lambda h: h.activation(out=qs[:, c, :T], in_=p_[:, :T], func=AF.Copy, scale=0.125), rd=[pb_], wr=[qsb])
                    kb.dma("sp", fm(g.Q, t0, T), qs[:], rd=[qsb])
                    for c in range(DC):
                        p_, pb_ = mm_fm(Wi, Wib, DC, 24 + c, hn, hnb, T)
                        kb.op("dve", lambda h: h.tensor_copy(out=ks[:, c, :T], in_=p_[:, :T]), rd=[pb_], wr=[ksb])
                    kb.dma("sp", fm(g.K, g.P + t0, T), ks[:], rd=[ksb])
                    for c in range(DC):
                        p_, pb_ = mm_fm(Wi, Wib, DC, 40 + c, hn, hnb, T)
                        kb.op("act", lambda h: h.activation(out=sgr[:, c, :T], in_=p_[:, :T], func=AF.Sigmoid), rd=[pb_], wr=[sgrb])
                    kb.dma("sp", fm(g.SGR, t0, T), sgr[:], rd=[sgrb])
                    for c in range(DC):
                        p_, pb_ = mm_fm(Wi, Wib, DC, 48 + c, hn, hnb, T)
                        kb.op("act", lambda h: h.activation(out=sga[:, c, :T], in_=p_[:, :T], func=AF.Sigmoid), rd=[pb_], wr=[sgab])
                    kb.dma("sp", fm(g.SGA, t0, T), sga[:], rd=[sgab])
                    kb.op("pool", lambda h: h.tensor_copy(out=hn2[:, :, :T], in_=hn[:, :, :T]), rd=[hnb], wr=[hn2b])
                    for m in range(T // MT):
                        for which in range(2):
                            tk, tkb = tok[which]
                            col0 = 3072 + which * 1024
                            NB = 256
                            for n in range(D // NB):
                                pt, pb = ps_next()
                                for c in range(DC):
                                    kb.op("pe", lambda h: h.matmul(pt[:MT, :NB], lhsT=hn2[:, c, m * MT:(m + 1) * MT],
                                                                   rhs=Wi[:, c, col0 + n * NB: col0 + (n + 1) * NB],
                                                                   start=(c == 0), stop=(c == DC - 1)),
                                          rd=[hn2b, Wib], wr=[pb], inc=(c == DC - 1))
                                kb.op("act", lambda h: h.activation(out=tk[:, n * NB:(n + 1) * NB], in_=pt[:MT, :NB], func=AF.Copy),
                                      rd=[pb], wr=[tkb])
                            dst = (g.ko if which == 0 else g.vo)
                            if which == 1:
                                tv, tvb = tokv[m % 2]
                                kb.op("dve", lambda h: h.tensor_copy(out=tv[:, :], in_=tk[:, :]), rd=[tkb], wr=[tvb])
                            for n in range(D // NB):
                                kb.dma("sp", dst[l, t0 + m * MT: t0 + (m + 1) * MT, n * NB:(n + 1) * NB], tk[:, n * NB:(n + 1) * NB], rd=[tkb])
                            if which == 1:
                                for n in range(D // 512):
                                    kb.dma("sp", g.V[g.P + t0 + m * MT: g.P + t0 + (m + 1) * MT, n * 512:(n + 1) * 512], tv[:, n * 512:(n + 1) * 512], rd=[tvb])
            kb.barrier()

    def phase_lru(l):
        with ExitStack() as st:
            WA = sb(st, "WA", [128, DC, 128], BF16); WAb = kb.buf()
            WX = sb(st, "WX", [128, DC, 128], BF16); WXb = kb.buf()
            kb.op("pool", lambda h: h.memset(WA[:], 0.0), wr=[WAb])
            kb.op("pool", lambda h: h.memset(WX[:], 0.0), wr=[WXb])
            for n in range(16):
                r0 = (n % 2) * 64
                kb.dma("pool", WA[r0:r0 + 64, n // 2, r0:r0 + 64], W["wa"][l, n], wr=[WAb])
                kb.dma("pool", WX[r0:r0 + 64, n // 2, r0:r0 + 64], W["wx"][l, n], wr=[WXb])
            for g, st in each_group(st):
                NSTR = 2 if g.S >= 1024 else 1
                SEG = min(1024, g.S) if NSTR == 2 else g.SEG
                nseg = g.S // SEG
                GT = min(512, SEG)
                NS = 2

                def mkslots(tag):
                    sl = []
                    for i in range(NS):
                        d = {}
                        for nm, shp, dt in (("ux", [128, SEG + 3], F32), ("xc", [128, SEG], F32), ("xcb", [128, SEG], BF16),
                                            ("r", [128, SEG], F32), ("i", [128, SEG], F32), ("m", [128, SEG], F32),
                                            ("gu", [128, SEG], BF16), ("h", [128, SEG], F32), ("y", [128, SEG], BF16)):
                            d[nm] = (sb(st, "%s%d%s%s" % (nm, i, tag, g.name), shp, dt), kb.buf())
                        sl.append(d)
                    return sl
                slots = [mkslots("s%d" % si) for si in range(NSTR)]
                h0 = h0b = None
                if g.P:
                    h0 = sb(st, "h0" + g.name, [128, DC], F32); h0b = kb.buf()
                    kb.dma("sp", h0[:], g.sh[l], wr=[h0b])

                def stream(chunks, sl):
                    k = 0
                    prev = None
                    for c in chunks:
                        uxd = g.UX[c * 128:(c + 1) * 128, :]
                        for sg_ in range(nseg):
                            t0 = sg_ * SEG
                            d = sl[k % NS]; k += 1
                            ux, uxb = d["ux"]; xc, xcb_ = d["xc"]; xcb, xcbb = d["xcb"]; r, rb = d["r"]; ii, ib = d["i"]
                            m, mb = d["m"]; gu, gub = d["gu"]; hh, hb = d["h"]; y, yb = d["y"]
                            if sg_ == 0:
                                if g.P:
                                    kb.dma("sp", ux[:, 0:3], g.sconv[l, :, c, :], wr=[uxb])
                                else:
                                    kb.op("pool", lambda h: h.memset(ux[:, 0:3], 0.0), wr=[uxb])
                                kb.dma("sp", ux[:, 3:SEG + 3], uxd[:, 0:SEG], wr=[uxb])
                            else:
                                kb.dma("sp", ux[:, :], uxd[:, t0 - 3:t0 + SEG], wr=[uxb])
                            kb.dma("sp", gu[:], g.GU[c * 128:(c + 1) * 128, t0:t0 + SEG], wr=[gub])
                            yield
                            kb.op("dve", lambda h: h.tensor_scalar(out=xc[:], in0=ux[:, 0:SEG], scalar1=convw(l, c, 0), scalar2=vcol(l, V_CONVB, c),
                                                                   op0=ALU.mult, op1=ALU.add), rd=[uxb, vecs_b], wr=[xcb_])
                            yield
                            for j in range(1, 4):
                                kb.op("dve", lambda h: h.scalar_tensor_tensor(out=xc[:], in0=ux[:, j:j + SEG], scalar=convw(l, c, j), in1=xc[:],
                                                                             op0=ALU.mult, op1=ALU.add), rd=[uxb, xcb_, vecs_b], wr=[xcb_])
                                yield
                            kb.op("pool", lambda h: h.tensor_copy(out=xcb[:], in_=xc[:]), rd=[xcb_], wr=[xcbb])
                            if sg_ == nseg - 1:
                                kb.dma("sp", g.co[l, :, c, :], ux[:, SEG:SEG + 3], rd=[uxb])
                            yield
                            for s0 in range(0, SEG, GT):
                                pt, pb = ps_next()
                                kb.op("pe", lambda h: h.matmul(pt[:, :GT], lhsT=WA[:, c, :], rhs=xcb[:, s0:s0 + GT], start=True, stop=True),
                                      rd=[WAb, xcbb], wr=[pb])
                                kb.op("act", lambda h: h.activation(out=r[:, s0:s0 + GT], in_=pt[:, :GT], func=AF.Sigmoid, bias=vcol(l, V_BA, c)),
                                      rd=[pb, vecs_b], wr=[rb])
                                yield
                                pt2, pb2 = ps_next()
                                kb.op("pe", lambda h: h.matmul(pt2[:, :GT], lhsT=WX[:, c, :], rhs=xcb[:, s0:s0 + GT], start=True, stop=True),
                                      rd=[WXb, xcbb], wr=[pb2])
                                kb.op("act", lambda h: h.activation(out=ii[:, s0:s0 + GT], in_=pt2[:, :GT], func=AF.Sigmoid, bias=vcol(l, V_BX, c)),
                                      rd=[pb2, vecs_b], wr=[ib])
                                yield
                            kb.op("act", lambda h: h.activation(out=r[:], in_=r[:], func=AF.Exp, scale=c8[:, l * 8 + c: l * 8 + c + 1]),
                                  rd=[rb, c8_b], wr=[rb])
                            yield
                            kb.op("pool", lambda h: h.tensor_tensor(out=m[:], in0=r[:], in1=r[:], op=ALU.mult), rd=[rb], wr=[mb])
                            yield
                            kb.op("dve", lambda h: h.tensor_scalar_min(out=m[:], in0=m[:], scalar1=1.0), rd=[mb], wr=[mb])
                            yield
                            kb.op("act", lambda h: h.activation(out=m[:], in_=m[:], func=AF.Sqrt, scale=-1.0, bias=1.0), rd=[mb], wr=[mb])
                            yield
                            if g.P == 0 and sg_ == 0:
                                kb.op("pool", lambda h: h.memset(m[:, 0:1], 1.0), rd=[mb], wr=[mb])
                            kb.op("pool", lambda h: h.tensor_tensor(out=ii[:], in0=ii[:], in1=xc[:], op=ALU.mult), rd=[ib, xcb_], wr=[ib])
                            yield
                            kb.op("pool", lambda h: h.tensor_tensor(out=ii[:], in0=ii[:], in1=m[:], op=ALU.mult), rd=[ib, mb], wr=[ib])
                            yield
                            if sg_ == 0:
                                init = h0[:, c:c + 1] if g.P else 0.0
                                ird = [h0b] if g.P else []
                            else:
                                init = prev[0][:, SEG - 1:SEG]
                                ird = [prev[1]]
                            kb.op("dve", lambda h: h.tensor_tensor_scan(out=hh[:], data0=r[:], data1=ii[:], initial=init, op0=ALU.mult, op1=ALU.add),
                                  rd=[rb, ib] + ird, wr=[hb])
                            prev = (hh, hb)
                            yield
                            if sg_ == nseg - 1:
                                kb.dma("sp", g.ho[l, :, c:c + 1], hh[:, SEG - 1:SEG], rd=[hb], allow_slow_non_contiguous=True)
                            kb.op("pool", lambda h: h.tensor_tensor(out=y[:], in0=hh[:], in1=gu[:], op=ALU.mult), rd=[hb, gub], wr=[yb])
                            kb.dma("sp", g.YR[c * 128:(c + 1) * 128, t0:t0 + SEG], y[:], rd=[yb])
                            yield

                per = DC // NSTR
                active = [stream(list(range(si * per, (si + 1) * per)), slots[si]) for si in range(NSTR)]
                while active:
                    for gen in list(active):
                        try:
                            next(gen)
                        except StopIteration:
                            active.remove(gen)
            kb.barrier()

    def phase_attn(l):
        with ExitStack() as st:
            past_dep = {}
            for g in groups:
                if g.P:
                    dmy = kb.buf()
                    for c in range(DC):
                        kb.dma("pool", g.K[c * 128:(c + 1) * 128, 0:g.P], g.ckT[l, c * 128:(c + 1) * 128, :], owner=dmy, max_dma_last_dim=4096)
                    for r0 in range(0, g.P, 128):
                        kb.dma("pool", g.V[r0:r0 + 128, :], g.cv[l, r0:r0 + 128, :], owner=dmy, max_dma_last_dim=4096)
                    past_dep[g.name] = ("d", dmy.ds["pool"], dmy.ds["pool"].cnt)
            for g, st in each_group(st):
                S_, P_ = g.S, g.P
                NK = P_ + S_
                QT = g.QT
                KBS = g.KBS
                npast = P_ // 128
                nkb_tot = npast + S_ // KBS
                if P_:
                    kb._wait("sp", past_dep[g.name])
                hp_sl = []
                for i in range(2):
                    d = {}
                    d["q"] = (sb(st, "aq%d%s" % (i, g.name), [128, S_], BF16), kb.buf())
                    d["k"] = (sb(st, "ak%d%s" % (i, g.name), [128, NK], BF16), kb.buf())
                    d["v"] = (sb(st, "av%d%s" % (i, g.name), [128, nkb_tot, 128], BF16), kb.buf())
                    hp_sl.append(d)
                es = [(sb(st, "e%d%s" % (i, g.name), [128, QT], F32), kb.buf()) for i in range(6)]
                xcs = [(sb(st, "xc%d%s" % (i, g.name), [128, QT], F32), kb.buf()) for i in range(2)]
                sps = [(sb(st, "sp%d%s" % (i, g.name), [128, QT], BF16), kb.buf()) for i in range(3)]
                ws = [(sb(st, "w%d%s" % (i, g.name), [128, QT], BF16), kb.buf()) for i in range(3)]
                sacc = [[(sb(st, "sa%d%d%s" % (i, j, g.name), [128, QT], BF16), kb.buf()) for j in range(2)] for i in range(2)]
                oev = [(sb(st, "oe%d%s" % (i, g.name), [64, QT], BF16), kb.buf()) for i in range(2)]
                ZP = [PS[0], PS[1], PS[6]]; CP = [PS[2], PS[3], PS[7]]; OP = [PS[4], PS[5]]

                def load_hp(hp):
                    d = hp_sl[hp % 2]
                    kb.dma("sp", d["q"][0][:], g.Q[hp * 128:(hp + 1) * 128, :], wr=[d["q"][1]])
                    kb.dma("sp", d["k"][0][:], g.K[hp * 128:(hp + 1) * 128, :], wr=[d["k"][1]])
                    vt, vb = d["v"]
                    if P_:
                        for b0 in range(0, npast, 8):
                            nb = min(8, npast - b0)
                            kb.dma("sp", vt[:, b0:b0 + nb, :],
                                   g.V[b0 * 128:(b0 + nb) * 128, hp * 128:(hp + 1) * 128].rearrange("(b p) d -> p b d", p=128), wr=[vb])
                    nnew = S_ // KBS
                    for b0 in range(0, nnew, 8):
                        nb = min(8, nnew - b0)
                        kb.dma("sp", vt[0:KBS, npast + b0: npast + b0 + nb, :],
                               g.V[P_ + b0 * KBS: P_ + (b0 + nb) * KBS, hp * 128:(hp + 1) * 128].rearrange("(b p) d -> p b d", p=KBS), wr=[vb])

                units = []
                seqi = 0
                for hp in range(NH // 2):
                    for qt in range(S_ // QT):
                        for ab in range(2):
                            blocks = []
                            nd = QT // KBS
                            for r in range(nd - 1, -1, -1):
                                blocks.append((npast + qt * nd + r, KBS, P_ + (qt * nd + r) * KBS, r))
                            for bq in range(qt * nd - 1, -1, -1):
                                blocks.append((npast + bq, KBS, P_ + bq * KBS, None))
                            for bp in range(npast - 1, -1, -1):
                                blocks.append((bp, 128, bp * 128, None))
                            for bi, (kbi, nk, kc0, r) in enumerate(blocks):
                                units.append(dict(hp=hp, qt=qt, ab=ab, kbi=kbi, nk=nk, kc0=kc0, r=r, pos=bi, first=(bi == 0),
                                                  last=(bi == len(blocks) - 1), seq=seqi))
                            seqi += 1

                loaded = set()

                def ensure(hp):
                    if hp < NH // 2 and hp not in loaded:
                        load_hp(hp)
                        loaded.add(hp)

                def qk_slices(u):
                    d = hp_sl[u["hp"] % 2]
                    qt_, qb = d["q"]; kt_, kbf = d["k"]
                    r0 = u["ab"] * 64
                    return (qt_[r0:r0 + 64, u["qt"] * QT:(u["qt"] + 1) * QT], kt_[r0:r0 + 64, u["kc0"]:u["kc0"] + u["nk"]], qb, kbf)

                def s_qk(i, u):
                    qsl, ksl, qb, kbf = qk_slices(u)
                    zt, zb = ZP[i % 3]
                    kb.op("pe", lambda h: h.matmul(zt[:u["nk"], :QT], lhsT=ksl, rhs=qsl, start=True, stop=True), rd=[qb, kbf], wr=[zb])

                def s_e(i, u):
                    nk = u["nk"]
                    zt, zb = ZP[i % 3]
                    e, eb = es[i % 6]
                    kb.op("act", lambda h: h.activation(out=e[:nk, :], in_=zt[:nk, :QT], func=AF.Exp), rd=[zb], wr=[eb])

                def s_sp(i, u):
                    nk = u["nk"]
                    e, eb = es[i % 6]
                    sp_, spb = sps[i % 3]
                    kb.op("act", lambda h: h.activation(out=sp_[:nk, :], in_=e[:nk, :], func=AF.Ln, bias=1.0), rd=[eb], wr=[spb])
                    if u["r"] is not None:
                        kb.op("dve", lambda h: h.tensor_tensor(out=sp_[:nk, :], in0=sp_[:nk, :], in1=mask_ap(u["r"], nk, QT), op=ALU.mult),
                              rd=[spb, cst_b], wr=[spb])
                    if not u["last"]:
                        so, sob = sacc[u["seq"] % 2][u["pos"] % 2]
                        sn, snb = sacc[u["seq"] % 2][(u["pos"] + 1) % 2]
                        if u["first"]:
                            if nk < 128:
                                kb.op("pool", lambda h: h.memset(sn[:], 0.0), wr=[snb])
                            kb.op("pool", lambda h: h.tensor_copy(out=sn[:nk, :], in_=sp_[:nk, :]), rd=[spb], wr=[snb])
                        else:
                            kb.op("pool", lambda h: h.tensor_tensor(out=sn[:, :], in0=so[:, :], in1=sp_[:, :], op=ALU.add),
                                  rd=[spb, sob], wr=[snb])

                def s_ct(i, u):
                    nk = u["nk"]
                    ct, cb = CP[i % 3]
                    sp_, spb = sps[i % 3]
                    kb.op("pe", lambda h: h.matmul(ct[:nk, :QT], lhsT=negtri[0:nk, 0:nk], rhs=sp_[:nk, :], start=True, stop=u["first"]),
                          rd=[spb, cst_b], wr=[cb], inc=u["first"])
                    if not u["first"]:
                        sa, sab = sacc[u["seq"] % 2][u["pos"] % 2]
                        kb.op("pe", lambda h: h.matmul(ct[:nk, :QT], lhsT=negones[:, 0:nk], rhs=sa[:, :], start=False, stop=True),
                              rd=[sab, cst_b], wr=[cb])

                def s_xc(i, u):
                    nk = u["nk"]
                    ct, cb = CP[i % 3]
                    xc_, xcb2 = xcs[i % 2]
                    kb.op("act", lambda h: h.activation(out=xc_[:nk, :], in_=ct[:nk, :QT], func=AF.Exp), rd=[cb], wr=[xcb2])

                def s_w(i, u):
                    nk = u["nk"]
                    xc_, xcb2 = xcs[i % 2]
                    e, eb = es[i % 6]
                    w_, wb = ws[i % 3]
                    kb.op("dve", lambda h: h.tensor_tensor(out=w_[:nk, :], in0=e[:nk, :], in1=xc_[:nk, :], op=ALU.mult),
                          rd=[eb, xcb2], wr=[wb])
                    if u["r"] is not None:
                        kb.op("dve", lambda h: h.tensor_tensor(out=w_[:nk, :], in0=w_[:nk, :], in1=mask_ap(u["r"], nk, QT), op=ALU.mult),
                              rd=[wb, cst_b], wr=[wb])

                def s_pv(i, u):
                    d = hp_sl[u["hp"] % 2]
                    vt, vb = d["v"]
                    nk = u["nk"]
                    c0 = u["ab"] * 64
                    ot, ob = OP[u["seq"] % 2]
                    w_, wb = ws[i % 3]
                    kb.op("pe", lambda h: h.matmul(ot[:64, :QT], lhsT=vt[0:nk, u["kbi"], c0:c0 + 64], rhs=w_[:nk, :],
                                                   start=u["first"], stop=u["last"]), rd=[vb, wb], wr=[ob])
                    if u["last"]:
                        oe, oeb = oev[u["seq"] % 2]
                        kb.op("dve", lambda h: h.tensor_copy(out=oe[:, :], in_=ot[:64, :QT]), rd=[ob], wr=[oeb])
                        hrow = u["hp"] * 128 + c0
                        kb.dma("sp", g.OT[hrow:hrow + 64, u["qt"] * QT:(u["qt"] + 1) * QT], oe[:, :], rd=[oeb])

                n = len(units)
                ensure(0)

                def at(j):
                    return units[j] if 0 <= j < n else None
                for k in range(n + 7):
                    u = at(k)
                    u6 = at(k - 6)
                    if u6 is not None and u6["first"] and u6["qt"] == 0 and u6["ab"] == 0:
                        ensure(u6["hp"] + 1)
                    if u is not None:
                        s_qk(k, u)
                    if at(k - 4) is not None:
                        s_xc(k - 4, at(k - 4))
                    if at(k - 3) is not None:
                        s_ct(k - 3, at(k - 3))
                    if at(k - 1) is not None:
                        s_e(k - 1, at(k - 1))
                    if at(k - 5) is not None:
                        s_w(k - 5, at(k - 5))
                    if at(k - 2) is not None:
                        s_sp(k - 2, at(k - 2))
                    if at(k - 6) is not None:
                        s_pv(k - 6, at(k - 6))
                kb.barrier()

    def phase_merge(l):
        with ExitStack() as st:
            Wr, Wrb = load_w(st, "Wr", W["wbr"][l], DC, D)
            Wa, Wab = load_w(st, "Wa", W["wba"][l], DC, D)
            Wo, Wob = load_w(st, "Wo", W["wout"][l], DC, D)
            for g, st in each_group(st):
                T = min(512, g.S)
                nt = g.S // T
                sl = []
                for i in range(2):
                    d = {}
                    for nm, dt in (("x", F32), ("yr", BF16), ("ot", BF16), ("sgr", F32), ("sga", F32)):
                        d[nm] = (sb(st, "%s%d%s" % (nm, i, g.name), [128, DC, T], dt), kb.buf())
                    sl.append(d)
                mg = sb(st, "mg" + g.name, [128, DC, T], BF16); mgb = kb.buf()
                m1s = [(sb(st, "m1%d%s" % (i, g.name), [128, T], F32), kb.buf()) for i in range(2)]
                m2s = [(sb(st, "m2%d%s" % (i, g.name), [128, T], F32), kb.buf()) for i in range(2)]

                def load(it):
                    d = sl[it % 2]
                    t0 = it * T
                    for nm, src in (("x", g.X), ("yr", g.YR), ("ot", g.OT), ("sgr", g.SGR), ("sga", g.SGA)):
                        kb.dma("sp", d[nm][0][:], fm(src, t0, T), wr=[d[nm][1]])
                load(0)
                for it in range(nt):
                    if it + 1 < nt:
                        load(it + 1)
                    d = sl[it % 2]
                    x, xb = d["x"]
                    for o in range(DC):
                        p1, p1b = mm_fm(Wr, Wrb, DC, o, d["yr"][0], d["yr"][1], T)
                        p2, p2b = mm_fm(Wa, Wab, DC, o, d["ot"][0], d["ot"][1], T)
                        m1, m1b = m1s[o % 2]; m2, m2b = m2s[o % 2]
                        kb.op("dve", lambda h: h.tensor_tensor(out=m1[:, :T], in0=p1[:, :T], in1=d["sgr"][0][:, o, :T], op=ALU.mult),
                              rd=[p1b, d["sgr"][1]], wr=[m1b])
                        kb.op("dve", lambda h: h.tensor_tensor(out=m2[:, :T], in0=p2[:, :T], in1=d["sga"][0][:, o, :T], op=ALU.mult),
                              rd=[p2b, d["sga"][1]], wr=[m2b])
                        kb.op("pool", lambda h: h.tensor_tensor(out=mg[:, o, :T], in0=m1[:, :T], in1=m2[:, :T], op=ALU.add),
                              rd=[m1b, m2b], wr=[mgb])
                    for o in range(DC):
                        p3, p3b = mm_fm(Wo, Wob, DC, o, mg, mgb, T)
                        kb.op("dve", lambda h: h.tensor_tensor(out=x[:, o, :T], in0=p3[:, :T], in1=x[:, o, :T], op=ALU.add),
                              rd=[p3b, xb], wr=[xb])
                    kb.dma("sp", fm(g.X, it * T, T), x[:], rd=[xb])
            kb.barrier()

    def phase_ple(l):
        last = (l == DEPTH - 1)
        with ExitStack() as st:
            Pg, Pgb = load_w(st, "Pg", W["pg"][l], DC, D)
            Pp, Ppb = load_w(st, "Pp", W["pp"][l], 2, D)
            for g, st in each_group(st):
                T = min(512, g.S)
                nt = g.S // T
                xs = [(sb(st, "x%d%s" % (i, g.name), [128, DC, T], F32), kb.buf()) for i in range(2)]
                pbs = [(sb(st, "pb%d%s" % (i, g.name), [128, 2, T], BF16), kb.buf()) for i in range(2)]
                hn = sb(st, "hn" + g.name, [128, DC, T], BF16); hnb = kb.buf()
                xsq = sb(st, "xsq" + g.name, [128, DC, T], BF16); xsqb = kb.buf()
                rstd = sb(st, "rstd" + g.name, [128, T], F32); rstdb = kb.buf()
                sgs = [(sb(st, "sg%d%s" % (i, g.name), [128, T], F32), kb.buf()) for i in range(2)]
                yo = sb(st, "yo" + g.name, [128, DC, T], F32); yob = kb.buf()

                def load(it):
                    t0 = it * T
                    kb.dma("sp", xs[it % 2][0][:], fm(g.X, t0, T), wr=[xs[it % 2][1]])
                    kb.dma("pool", pbs[it % 2][0][:], fm(g.pT[l], t0, T), wr=[pbs[it % 2][1]])
                load(0)
                for it in range(nt):
                    if it + 1 < nt:
                        load(it + 1)
                    x, xb = xs[it % 2]
                    pb_, pbb = pbs[it % 2]
                    rmsnorm(x, xb, hn, hnb, xsq, xsqb, rstd, rstdb, T, lambda c: vcol(l, V_NPLE, c))
                    for o in range(DC):
                        p1, p1b = mm_fm(Pg, Pgb, DC, o, hn, hnb, T)
                        p2, p2b = mm_fm(Pp, Ppb, 2, o, pb_, pbb, T)
                        sg, sgb = sgs[o % 2]
                        kb.op("act", lambda h: h.activation(out=sg[:, :T], in_=p1[:, :T], func=AF.Sigmoid), rd=[p1b], wr=[sgb])
                        kb.op("dve", lambda h: h.tensor_tensor(out=sg[:, :T], in0=p2[:, :T], in1=sg[:, :T], op=ALU.mult), rd=[p2b, sgb], wr=[sgb])
                        kb.op("pool", lambda h: h.tensor_tensor(out=x[:, o, :T], in0=x[:, o, :T], in1=sg[:, :T], op=ALU.add), rd=[xb, sgb], wr=[xb])
                    if not last:
                        kb.dma("sp", fm(g.X, it * T, T), x[:], rd=[xb])
                    else:
                        rmsnorm(x, xb, None, None, xsq, xsqb, rstd, rstdb, T, None)
                        for c in range(DC):
                            kb.op("dve", lambda h: h.scalar_tensor_tensor(out=yo[:, c, :T], in0=x[:, c, :T], scalar=fin(c), in1=rstd[:, :T],
                                                                         op0=ALU.mult, op1=ALU.mult), rd=[xb, rstdb, vecs_b], wr=[yob])
                        kb.dma("sp", fm(g.yT, it * T, T), yo[:], rd=[yob])
            kb.barrier()

    import os
    nph = int(os.environ.get("KDEBUG_PHASES", "999"))
    plist = []
    for l in range(DEPTH):
        plist += [lambda l=l: phase_ffn(l, W["f1g"], W["f1u"], W["f1d"], V_NF1, first=(l == 0)),
                  lambda l=l: phase_in(l), lambda l=l: phase_lru(l), lambda l=l: phase_attn(l), lambda l=l: phase_merge(l),
                  lambda l=l: phase_ffn(l, W["f2g"], W["f2u"], W["f2d"], V_NF2, first=False), lambda l=l: phase_ple(l)]
    for ph in plist[:nph]:
        ph()
    kb.barrier()


def _consts():
    c = np.zeros((128, 3 * 128 + 4 * 512), np.float32)
    j = np.arange(128)[:, None]
    s = np.arange(128)[None, :]
    c[:, 0:128] = -(j >= s).astype(np.float32)
    c[:, 128:256] = -1.0
    c[:, 256:384] = 1.0 / 1024.0
    t = np.arange(512)[None, :]
    for r in range(4):
        c[:, 384 + r * 512: 384 + (r + 1) * 512] = (t > (r * 128 + j)).astype(np.float32)
    return c


def _pc(v):
    return np.ascontiguousarray(np.asarray(v, np.float32).reshape(8, 128).T)


_NC_CACHE = {}


def kernel(**inp):
    inp = {k: np.asarray(v) for k, v in inp.items()}
    xp = inp["x_prompt"]
    B, S = xp.shape[0], xp.shape[1]
    if S not in _NC_CACHE:
        _NC_CACHE[S] = build(S)
    nc = _NC_CACHE[S]
    vecs = np.zeros((128, NV_L * DEPTH + 8), np.float32)
    for l in range(DEPTH):
        o = l * NV_L
        for k, nm in enumerate(("norm_ffn1", "norm_mix", "conv_b", "lru_b_a", "lru_b_x", "lru_lambda", "norm_ffn2", "norm_ple")):
            vecs[:, o + k * 8: o + k * 8 + 8] = _pc(inp[nm][l].reshape(-1))
        cw = inp["conv_w"][l]
        vecs[:, o + 64: o + 96] = np.ascontiguousarray(cw.reshape(4, 8, 128).transpose(2, 1, 0)).reshape(128, 32)
    vecs[:, DEPTH * NV_L:] = _pc(inp["final_norm"])
    cst = _consts()
    shared = {"vecs": vecs, "cst": cst}
    for nm in ("ffn1_w_gate", "ffn1_w_up", "ffn1_w_down", "ffn2_w_gate", "ffn2_w_up", "ffn2_w_down", "w_in", "lru_w_a", "lru_w_x",
               "w_branch_rnn", "w_branch_attn", "w_out", "ple_w_gate", "ple_w_proj"):
        shared[nm] = np.ascontiguousarray(inp[nm], dtype=np.float32)
    n_cores = 8
    in_maps = []
    for c in range(n_cores):
        b = (c * B) // n_cores
        m = dict(shared)
        m["xTp"] = np.ascontiguousarray(xp[b].T)
        m["pTp"] = np.ascontiguousarray(inp["p_prompt"][:, b].transpose(0, 2, 1))
        m["xTs"] = np.ascontiguousarray(inp["x_sample"][c].T)
        m["pTs"] = np.ascontiguousarray(inp["p_sample"][:, c].transpose(0, 2, 1))
        m["ckT"] = np.ascontiguousarray(inp["cache_k"][:, c].reshape(DEPTH, PAST, D).transpose(0, 2, 1))
        m["cv"] = np.ascontiguousarray(inp["cache_v"][:, c].reshape(DEPTH, PAST, D))
        m["sh"] = np.ascontiguousarray(inp["state_h"][:, c].reshape(DEPTH, 8, 128).transpose(0, 2, 1))
        m["sconv"] = np.ascontiguousarray(inp["state_conv"][:, c].reshape(DEPTH, 3, 8, 128).transpose(0, 3, 2, 1))
        in_maps.append(m)
    res = run_bass_kernel_spmd(nc, in_maps, core_ids=list(range(n_cores)))
    R = res.results
    DB = inp["x_sample"].shape[0]
    y_p = np.zeros((B, S, D), np.float32); k_p = np.zeros((DEPTH, B, S, NH, DH), np.float32); v_p = np.zeros_like(k_p)
    h_p = np.zeros((DEPTH, B, D), np.float32); c_p = np.zeros((DEPTH, B, 3, D), np.float32)
    y_s = np.zeros((DB, SD, D), np.float32); k_s = np.zeros((DEPTH, DB, SD, NH, DH), np.float32); v_s = np.zeros_like(k_s)
    h_s = np.zeros((DEPTH, DB, D), np.float32); c_s = np.zeros((DEPTH, DB, 3, D), np.float32)
    for c in range(n_cores):
        r = R[c]
        b = c // 2
        if c % 2 == 0:
            y_p[b] = r["yTp"].T
            k_p[:, b] = r["kp"].reshape(DEPTH, S, NH, DH)
            v_p[:, b] = r["vp"].reshape(DEPTH, S, NH, DH)
            h_p[:, b] = r["hp"].transpose(0, 2, 1).reshape(DEPTH, D)
            c_p[:, b] = r["cp"].transpose(0, 3, 2, 1).reshape(DEPTH, 3, D)
        y_s[c] = r["yTs"].T
        k_s[:, c] = r["ks"].reshape(DEPTH, SD, NH, DH)
        v_s[:, c] = r["vs"].reshape(DEPTH, SD, NH, DH)
        h_s[:, c] = r["hs"].transpose(0, 2, 1).reshape(DEPTH, D)
        c_s[:, c] = r["cs"].transpose(0, 3, 2, 1).reshape(DEPTH, 3, D)
    return (y_p, y_s, k_p, v_p, h_p, c_p, k_s, v_s, h_s, c_s)
```
